# Optimizing a Trainium2 kernel written in Bass

```python
import math
import jax, jax.numpy as jnp
from jax import lax
import numpy as np

D_MODEL = 4096
BATCH = 2
SEQ = 4096
DEPTH = 2

N_BRANCHES = 4
BRANCH_WIDTH = D_MODEL // N_BRANCHES
HG_HEADS = 8
HG_DK = BRANCH_WIDTH // HG_HEADS
HG_DV = BRANCH_WIDTH // HG_HEADS
HG_CHUNK = 64
DS_HEADS = 8
DS_HEAD_DIM = BRANCH_WIDTH // DS_HEADS
DS_Q_LORA = 768
DS_KV_LORA = 512
DS_IDX_HEADS = 16
DS_IDX_DIM = 64
DS_TOPK_MAX = 256
DS_QBLOCK = 128
MB_HEADS = 16
MB_HEAD_DIM = BRANCH_WIDTH // MB_HEADS
MB_D_INNER = BRANCH_WIDTH
MB_STATE = 128
MB_GROUPS = 2
MB_CONV = 4
MB_CHUNK = 128
MB_CONV_DIM = MB_D_INNER + 2 * MB_GROUPS * MB_STATE
FX_HEADS = 8
FX_HEAD_DIM = BRANCH_WIDTH // FX_HEADS
FX_QBLOCK = 128
D_FF = 11008
FFN_CONV = 3
EPS = 1e-6

IN_SIZES = (BRANCH_WIDTH, BRANCH_WIDTH, BRANCH_WIDTH, BRANCH_WIDTH,
            DS_Q_LORA, DS_KV_LORA, DS_IDX_DIM, DS_IDX_HEADS,
            MB_D_INNER, MB_CONV_DIM, MB_HEADS,
            BRANCH_WIDTH, BRANCH_WIDTH, BRANCH_WIDTH, FX_HEADS)
D_IN = sum(IN_SIZES)

kernel_name = 'hybrid_hgrn2_dsa_mamba2_fox_gated_merge'


def rms_norm(x, gain):
    xf = x.astype(jnp.float32)
    y = xf * lax.rsqrt(jnp.mean(xf * xf, axis=-1, keepdims=True) + EPS)
    return (y * gain.astype(jnp.float32)).astype(x.dtype)


def causal_dwconv(x, w):
    K, S = w.shape[0], x.shape[1]
    xp = jnp.pad(x, ((0, 0), (K - 1, 0), (0, 0)))
    w = w.astype(x.dtype)
    return sum(xp[:, k:k + S] * w[k] for k in range(K))


def split_columns(p):
    offsets = [int(o) for o in np.cumsum(IN_SIZES)[:-1]]
    return jnp.split(p, offsets, axis=-1)


def hgrn_lower_bounds(logits):
    p = jax.nn.softmax(logits.astype(jnp.float32), axis=0)
    return jnp.cumsum(p, axis=0) - p[0]


def hgrn2_mixer(q, f_logit, i, g, lower_bound, norm_gain):
    f32 = jnp.float32
    B, S, _ = q.shape
    nc = S // HG_CHUNK
    q = jax.nn.silu(q.astype(f32))
    v = jax.nn.silu(i.astype(f32))
    lb = lower_bound.astype(f32)
    f = lb + (1.0 - lb) * jax.nn.sigmoid(f_logit.astype(f32))
    log_f = jnp.log(f)
    k = 1.0 - f

    def to_chunks(t, d):
        return t.reshape(B, nc, HG_CHUNK, HG_HEADS, d).transpose(1, 0, 3, 2, 4)

    qc, kc, gc = to_chunks(q, HG_DK), to_chunks(k, HG_DK), to_chunks(log_f, HG_DK)
    vc = to_chunks(v, HG_DV)
    causal = jnp.tril(jnp.ones((HG_CHUNK, HG_CHUNK), bool))

    def step(state, inp):
        qb, kb, vb, gb = inp
        b = jnp.cumsum(gb, axis=2)
        inter = jnp.einsum('bhtk,bhkv->bhtv', qb * jnp.exp(b), state)
        diff = b[:, :, :, None, :] - b[:, :, None, :, :]
        decay = jnp.exp(jnp.where(causal[:, :, None], diff, -jnp.inf))
        scores = jnp.einsum('bhtk,bhtsk,bhsk->bhts', qb, decay, kb)
        intra = jnp.einsum('bhts,bhsv->bhtv', scores, vb)
        b_last = b[:, :, -1:, :]
        new_state = (jnp.exp(b_last[:, :, 0, :, None]) * state
                     + jnp.einsum('bhsk,bhsv->bhkv', kb * jnp.exp(b_last - b), vb))
        return new_state, inter + intra

    state0 = jnp.zeros((B, HG_HEADS, HG_DK, HG_DV), f32)
    _, o = lax.scan(step, state0, (qc, kc, vc, gc))
    o = o.transpose(1, 0, 3, 2, 4).reshape(B, S, HG_HEADS, HG_DV)
    gate = jax.nn.sigmoid(g.astype(f32)).reshape(B, S, HG_HEADS, HG_DV)
    o = rms_norm(o * gate, norm_gain.reshape(HG_HEADS, HG_DV))
    return o.reshape(B, S, BRANCH_WIDTH)


def dsa_mixer(c_q, c_kv, k_idx, w_idx, q_norm, kv_norm, w_uq, w_iq, w_uk, w_uv):
    f32 = jnp.float32
    B, S, _ = c_q.shape
    topk = min(DS_TOPK_MAX, S // 4)
    nb = S // DS_QBLOCK
    c_q = rms_norm(c_q.astype(f32), q_norm)
    c_kv = rms_norm(c_kv.astype(f32), kv_norm)
    k_idx = k_idx.astype(f32)
    q = (c_q @ w_uq.astype(f32)).reshape(B, S, DS_HEADS, DS_HEAD_DIM)
    q_idx = (c_q @ w_iq.astype(f32)).reshape(B, S, DS_IDX_HEADS, DS_IDX_DIM)
    w_idx = w_idx.astype(f32) * (DS_IDX_HEADS ** -0.5 * DS_IDX_DIM ** -0.5)
    w_uk = w_uk.astype(f32).reshape(DS_KV_LORA, DS_HEADS, DS_HEAD_DIM)
    w_uv = w_uv.astype(f32).reshape(DS_KV_LORA, DS_HEADS, DS_HEAD_DIM)
    q_lat = jnp.einsum('bshd,chd->bshc', q, w_uk) * DS_HEAD_DIM ** -0.5
    key_pos = jnp.arange(S)

    def blockify(t):
        return t.reshape((B, nb, DS_QBLOCK) + t.shape[2:]).swapaxes(0, 1)

    def attend_block(args):
        qi, wi, ql, t0 = args
        t = t0 + jnp.arange(DS_QBLOCK)
        idx_logits = jax.nn.relu(jnp.einsum('bqhd,bsd->bqhs', qi, k_idx))
        score = jnp.einsum('bqhs,bqh->bqs', idx_logits, wi)
        score = jnp.where(key_pos[None, None, :] <= t[None, :, None], score, -jnp.inf)
        _, sel = lax.top_k(score, topk)
        valid = sel <= t[None, :, None]
        kv_sel = jax.vmap(lambda c, ix: c[ix])(c_kv, sel)
        logits = jnp.einsum('bqhc,bqkc->bqhk', ql, kv_sel)
        logits = jnp.where(valid[:, :, None, :], logits, -jnp.inf)
        p = jax.nn.softmax(logits, axis=-1)
        return jnp.einsum('bqhk,bqkc->bqhc', p, kv_sel)

    o_lat = lax.map(attend_block, (blockify(q_idx), blockify(w_idx), blockify(q_lat),
                                   jnp.arange(nb) * DS_QBLOCK))
    o_lat = o_lat.swapaxes(0, 1).reshape(B, S, DS_HEADS, DS_KV_LORA)
    o = jnp.einsum('bshc,chv->bshv', o_lat, w_uv)
    return o.reshape(B, S, BRANCH_WIDTH)


def segsum(a):
    T = a.shape[-1]
    cs = jnp.cumsum(a, axis=-1)
    diff = cs[..., :, None] - cs[..., None, :]
    return jnp.where(jnp.tril(jnp.ones((T, T), bool)), diff, -jnp.inf)


def ssd_chunked(x, a, b, c):
    Bsz, S, H, P = x.shape
    nc = S // MB_CHUNK
    x = x.reshape(Bsz, nc, MB_CHUNK, H, P)
    b = b.reshape(Bsz, nc, MB_CHUNK, H, -1)
    c = c.reshape(Bsz, nc, MB_CHUNK, H, -1)
    a = a.reshape(Bsz, nc, MB_CHUNK, H).transpose(0, 3, 1, 2)
    a_cs = jnp.cumsum(a, axis=-1)
    Lmat = jnp.exp(segsum(a))
    y_diag = jnp.einsum('bclhn,bcshn,bhcls,bcshp->bclhp', c, b, Lmat, x)
    decay_states = jnp.exp(a_cs[..., -1:] - a_cs)
    states = jnp.einsum('bclhn,bhcl,bclhp->bchpn', b, decay_states, x)
    states = jnp.concatenate([jnp.zeros_like(states[:, :1]), states], axis=1)
    chunk_decay = jnp.exp(segsum(jnp.pad(a_cs[..., -1], ((0, 0), (0, 0), (1, 0)))))
    states = jnp.einsum('bhzc,bchpn->bzhpn', chunk_decay, states)[:, :-1]
    y_off = jnp.einsum('bclhn,bchpn,bhcl->bclhp', c, states, jnp.exp(a_cs))
    return (y_diag + y_off).reshape(Bsz, S, H, P)


def mamba2_mixer(z, xbc, dt, conv_w, conv_b, dt_bias, a_log, d_skip, norm_gain):
    f32 = jnp.float32
    B, S, _ = z.shape
    rep = MB_HEADS // MB_GROUPS
    xbc = jax.nn.silu(causal_dwconv(xbc.astype(f32), conv_w) + conv_b.astype(f32))
    xs, bm, cm = jnp.split(xbc, [MB_D_INNER, MB_D_INNER + MB_GROUPS * MB_STATE], axis=-1)
    xs = xs.reshape(B, S, MB_HEADS, MB_HEAD_DIM)
    bm = jnp.repeat(bm.reshape(B, S, MB_GROUPS, MB_STATE), rep, axis=2)
    cm = jnp.repeat(cm.reshape(B, S, MB_GROUPS, MB_STATE), rep, axis=2)
    dt = jax.nn.softplus(dt.astype(f32) + dt_bias.astype(f32))
    a = -jnp.exp(a_log.astype(f32))
    y = ssd_chunked(xs * dt[..., None], a * dt, bm, cm) + d_skip.astype(f32)[:, None] * xs
    y = y.reshape(B, S, MB_D_INNER) * jax.nn.silu(z.astype(f32))
    y = rms_norm(y.reshape(B, S, MB_GROUPS, MB_D_INNER // MB_GROUPS),
                 norm_gain.reshape(MB_GROUPS, MB_D_INNER // MB_GROUPS))
    return y.reshape(B, S, MB_D_INNER)


def fox_mixer(q, k, v, f_logit, f_bias):
    f32 = jnp.float32
    B, S, _ = q.shape
    nb = S // FX_QBLOCK
    q = q.astype(f32).reshape(B, S, FX_HEADS, FX_HEAD_DIM) * FX_HEAD_DIM ** -0.5
    k = k.astype(f32).reshape(B, S, FX_HEADS, FX_HEAD_DIM)
    v = v.astype(f32).reshape(B, S, FX_HEADS, FX_HEAD_DIM)
    log_f = jax.nn.log_sigmoid(f_logit.astype(f32) + f_bias.astype(f32))
    cum = jnp.cumsum(log_f, axis=1)
    cum_k = cum.transpose(0, 2, 1)
    key_pos = jnp.arange(S)

    def blockify(t):
        return t.reshape((B, nb, FX_QBLOCK) + t.shape[2:]).swapaxes(0, 1)

    def attend_block(args):
        qb, cb, t0 = args
        t = t0 + jnp.arange(FX_QBLOCK)
        logits = (jnp.einsum('bqhd,bshd->bhqs', qb, k)
                  + cb.transpose(0, 2, 1)[..., None] - cum_k[:, :, None, :])
        logits = jnp.where(key_pos[None, :] <= t[:, None], logits, -jnp.inf)
        p = jax.nn.softmax(logits, axis=-1)
        return jnp.einsum('bhqs,bshd->bqhd', p, v)

    o = lax.map(attend_block, (blockify(q), blockify(cum), jnp.arange(nb) * FX_QBLOCK))
    return o.swapaxes(0, 1).reshape(B, S, BRANCH_WIDTH)


def conv_ffn(h, w_gate, w_up, conv_w, w_down):
    gate = causal_dwconv(h @ w_gate, conv_w)
    return (jax.nn.silu(gate) * (h @ w_up)) @ w_down


def setup_inputs(seed: int = 0) -> dict:
    key = jax.random.key(seed)
    keys = iter(jax.random.split(key, 32))
    L = DEPTH

    def nrm(shape, scale):
        return jax.random.normal(next(keys), shape, jnp.float32) * scale

    def gain(shape):
        return 1.0 + nrm(shape, 0.02)

    x = nrm((BATCH, SEQ, D_MODEL), 1.0)
    attn_norm = gain((L, D_MODEL))
    ffn_norm = gain((L, D_MODEL))
    final_norm = gain((D_MODEL,))
    w_in = nrm((L, D_MODEL, D_IN), D_MODEL ** -0.5)
    hgrn_lb_logits = nrm((L, HG_HEADS * HG_DK), 0.5)
    hgrn_norm = gain((L, BRANCH_WIDTH))
    dsa_q_norm = gain((L, DS_Q_LORA))
    dsa_kv_norm = gain((L, DS_KV_LORA))
    dsa_w_uq = nrm((L, DS_Q_LORA, DS_HEADS * DS_HEAD_DIM), DS_Q_LORA ** -0.5)
    dsa_w_iq = nrm((L, DS_Q_LORA, DS_IDX_HEADS * DS_IDX_DIM), DS_Q_LORA ** -0.5)
    dsa_w_uk = nrm((L, DS_KV_LORA, DS_HEADS * DS_HEAD_DIM), DS_KV_LORA ** -0.5)
    dsa_w_uv = nrm((L, DS_KV_LORA, DS_HEADS * DS_HEAD_DIM), DS_KV_LORA ** -0.5)
    ssm_conv_w = nrm((L, MB_CONV, MB_CONV_DIM), MB_CONV ** -0.5)
    ssm_conv_b = nrm((L, MB_CONV_DIM), 0.02)
    dt0 = jnp.exp(jax.random.uniform(next(keys), (L, MB_HEADS), jnp.float32,
                                     minval=math.log(1e-3), maxval=math.log(1e-1)))
    ssm_dt_bias = dt0 + jnp.log(-jnp.expm1(-dt0))
    ssm_a_log = jnp.log(jax.random.uniform(next(keys), (L, MB_HEADS), jnp.float32,
                                           minval=1.0, maxval=16.0))
    ssm_d = gain((L, MB_HEADS))
    ssm_norm = gain((L, MB_D_INNER))
    fox_f_bias = 2.0 + nrm((L, FX_HEADS), 0.5)
    w_gate = nrm((L, N_BRANCHES, D_MODEL, D_MODEL), D_MODEL ** -0.5)
    w_branch = nrm((L, N_BRANCHES, BRANCH_WIDTH, D_MODEL), BRANCH_WIDTH ** -0.5)
    w_out = nrm((L, D_MODEL, D_MODEL), D_MODEL ** -0.5)
    ffn_w_gate = nrm((L, D_MODEL, D_FF), D_MODEL ** -0.5)
    ffn_w_up = nrm((L, D_MODEL, D_FF), D_MODEL ** -0.5)
    ffn_conv = nrm((L, FFN_CONV, D_FF), FFN_CONV ** -0.5)
    ffn_w_down = nrm((L, D_FF, D_MODEL), D_FF ** -0.5)
    return {'x': x, 'attn_norm': attn_norm, 'ffn_norm': ffn_norm, 'final_norm': final_norm,
            'w_in': w_in, 'hgrn_lb_logits': hgrn_lb_logits, 'hgrn_norm': hgrn_norm,
            'dsa_q_norm': dsa_q_norm, 'dsa_kv_norm': dsa_kv_norm, 'dsa_w_uq': dsa_w_uq,
            'dsa_w_iq': dsa_w_iq, 'dsa_w_uk': dsa_w_uk, 'dsa_w_uv': dsa_w_uv,
            'ssm_conv_w': ssm_conv_w, 'ssm_conv_b': ssm_conv_b, 'ssm_dt_bias': ssm_dt_bias,
            'ssm_a_log': ssm_a_log, 'ssm_d': ssm_d, 'ssm_norm': ssm_norm,
            'fox_f_bias': fox_f_bias, 'w_gate': w_gate, 'w_branch': w_branch, 'w_out': w_out,
            'ffn_w_gate': ffn_w_gate, 'ffn_w_up': ffn_w_up, 'ffn_conv': ffn_conv,
            'ffn_w_down': ffn_w_down}


def reference(x, attn_norm, ffn_norm, final_norm, w_in, hgrn_lb_logits, hgrn_norm,
              dsa_q_norm, dsa_kv_norm, dsa_w_uq, dsa_w_iq, dsa_w_uk, dsa_w_uv,
              ssm_conv_w, ssm_conv_b, ssm_dt_bias, ssm_a_log, ssm_d, ssm_norm,
              fox_f_bias, w_gate, w_branch, w_out, ffn_w_gate, ffn_w_up, ffn_conv, ffn_w_down):
    lower_bounds = hgrn_lower_bounds(hgrn_lb_logits)
    for l in range(DEPTH):
        h = rms_norm(x, attn_norm[l])
        (hq, hf, hi, hg, dcq, dckv, dki, dwi, mz, mxbc, mdt,
         fq, fk, fv, ff) = split_columns(h @ w_in[l])
        y_a = hgrn2_mixer(hq, hf, hi, hg, lower_bounds[l], hgrn_norm[l])
        y_b = dsa_mixer(dcq, dckv, dki, dwi, dsa_q_norm[l], dsa_kv_norm[l],
                        dsa_w_uq[l], dsa_w_iq[l], dsa_w_uk[l], dsa_w_uv[l])
        y_c = mamba2_mixer(mz, mxbc, mdt, ssm_conv_w[l], ssm_conv_b[l], ssm_dt_bias[l],
                           ssm_a_log[l], ssm_d[l], ssm_norm[l])
        y_d = fox_mixer(fq, fk, fv, ff, fox_f_bias[l])
        merged = jnp.zeros_like(x)
        for n, y_n in enumerate((y_a, y_b, y_c, y_d)):
            merged = merged + jax.nn.sigmoid(h @ w_gate[l, n]) * (y_n.astype(x.dtype) @ w_branch[l, n])
        x = x + merged @ w_out[l]
        h = rms_norm(x, ffn_norm[l])
        x = x + conv_ffn(h, ffn_w_gate[l], ffn_w_up[l], ffn_conv[l], ffn_w_down[l])
    return rms_norm(x, final_norm)
```

```python
import numpy as np
from contextlib import ExitStack
import concourse.bass as bass
import concourse.mybir as mybir
from concourse.bass_utils import run_bass_kernel_spmd

F32 = mybir.dt.float32
BF16 = mybir.dt.bfloat16
AF = mybir.ActivationFunctionType
ALU = mybir.AluOpType
NCORES = 8
EPS = 1e-6
NEG = -30000.0


class Sched:
    ENG = ["pe", "act", "dve", "pool", "sp"]
    NDS = 24

    def __init__(self, nc, es):
        self.nc = nc
        self.es_outer = es
        self.esem = {e: es.enter_context(nc.semaphore("es_" + e)) for e in self.ENG}
        self.ecnt = {e: 0 for e in self.ENG}
        self.dsem = [es.enter_context(nc.semaphore("ds%d" % i)) for i in range(self.NDS)]
        self.dcnt = [0] * self.NDS
        self.dnext = {"sp": 0, "pool": 0}
        self.ccsem = es.enter_context(nc.semaphore("ccs"))
        self.cccnt = 0
        self.waited = {e: {} for e in self.ENG}
        self.reset_stage()

    def reset_stage(self):
        self.prog = {e: [] for e in self.ENG}
        self.lastw = {}
        self.readers = {}

    def semobj(self, sk):
        if sk[0] == "e":
            return self.esem[sk[1]]
        if sk[0] == "d":
            return self.dsem[sk[1]]
        return self.ccsem

    @staticmethod
    def key(x):
        if isinstance(x, (str, tuple)):
            return x
        if hasattr(x, "tensor"):
            return x.tensor.name
        return x.name

    def _deps(self, eng, R, W):
        deps = []
        for k in R:
            t = self.lastw.get(k)
            if t:
                deps.append(t)
        for k in W:
            t = self.lastw.get(k)
            if t:
                deps.append(t)
            deps.extend(self.readers.get(k, {}).values())
        waits = []
        for sk, v in deps:
            if eng == "pe" and sk == ("e", "pe"):
                continue
            if self.waited[eng].get(sk, 0) >= v:
                continue
            self.waited[eng][sk] = v
            waits.append((sk, v))
        return waits

    def _commit(self, tok, R, W):
        for k in R:
            self.readers.setdefault(k, {})[tok[0]] = tok
        for k in W:
            self.lastw[k] = tok
            self.readers[k] = {}

    def op(self, eng, fn, R=(), W=()):
        R = [self.key(x) for x in R]
        W = [self.key(x) for x in W]
        waits = self._deps(eng, R, W)
        self.ecnt[eng] += 1
        tok = (("e", eng), self.ecnt[eng])
        self.prog[eng].append((waits, fn, tok[0], 1))
        self._commit(tok, R, W)
        return tok

    def dma(self, q, out, in_, R=None, W=None, **kw):
        R = [self.key(x) for x in (R if R is not None else [in_])]
        W = [self.key(x) for x in (W if W is not None else [out])]
        R = [k for k in R if not (isinstance(k, str) and k.startswith("D_"))]
        W = [k for k in W if not (isinstance(k, str) and k.startswith("D_"))]
        waits = self._deps(q, R, W)
        half = self.NDS // 2
        slot = (0 if q == "sp" else half) + self.dnext[q]
        self.dnext[q] = (self.dnext[q] + 1) % half
        prev = self.dcnt[slot]
        sk = ("d", slot)
        if prev > 0 and self.waited[q].get(sk, 0) < prev:
            self.waited[q][sk] = prev
            waits.append((sk, prev))
        self.dcnt[slot] += 16
        tok = (sk, self.dcnt[slot])
        self.prog[q].append((waits, lambda e: e.dma_start(out=out, in_=in_, **kw), sk, 16))
        self._commit(tok, R, W)
        return tok

    def wait_tok(self, eng, tok):
        sk, v = tok
        if self.waited[eng].get(sk, 0) >= v:
            return
        self.waited[eng][sk] = v
        self.prog[eng].append(([(sk, v)], None, None, 0))

    def drain_dma(self, engs=("sp", "pool")):
        for e in engs:
            for i in range(self.NDS):
                if self.dcnt[i] > 0:
                    self.wait_tok(e, (("d", i), self.dcnt[i]))

    def allgather(self, src, dst):
        for e in self.ENG:
            if e != "pool" and self.ecnt[e] > 0:
                self.wait_tok("pool", (("e", e), self.ecnt[e]))
        self.drain_dma(("pool",))
        R_, N_ = int(src.shape[0]), int(src.shape[1])
        esz = 4 if src.dtype == F32 else 2
        rb = max(1, min(R_, (512 * 1024) // (N_ * esz)))
        while R_ % rb:
            rb -= 1
        key = (rb, N_, str(src.dtype))
        if not hasattr(self, "agscr"):
            self.agscr = {}
        GR = 8
        if key not in self.agscr:
            i = len(self.agscr)
            self.agscr[key] = [(self.nc.dram_tensor("D_agm%d_%d" % (i, j), [4 * rb, N_], src.dtype),
                                self.nc.dram_tensor("D_ago%d_%d" % (i, j), [8 * rb, N_], src.dtype)) for j in range(2 * GR)]
            self.agtok = getattr(self, "agtok", {})
        g1 = [[0, 1, 2, 3], [4, 5, 6, 7]]
        g2 = [[0, 4], [1, 5], [2, 6], [3, 7]]
        dstv = dst.ap().rearrange("(r k) n -> r k n", k=R_)
        nchunk = R_ // rb

        def coll(a_ap, b_t, grp):
            self.cccnt += 1
            self.prog["pool"].append(([], lambda e: e.collective_compute(
                "AllGather", ALU.bypass, replica_groups=grp, ins=[a_ap.opt()], outs=[b_t.ap().opt()]), ("c",), 1))
            return self.cccnt

        def ccwait(v):
            if self.waited["pool"].get(("c",), 0) < v:
                self.waited["pool"][("c",)] = v
                self.prog["pool"].append(([(("c",), v)], None, None, 0))
        for g0 in range(0, nchunk, GR):
            cis = list(range(g0, min(g0 + GR, nchunk)))
            v1 = {}
            for ci in cis:
                mid, out = self.agscr[key][ci % (2 * GR)]
                tokk = (key, ci % (2 * GR))
                if tokk in self.agtok:
                    self.wait_tok("pool", self.agtok[tokk])
                v1[ci] = coll(src.ap()[ci * rb:(ci + 1) * rb, :], mid, g1)
            v2 = {}
            for ci in cis:
                mid, out = self.agscr[key][ci % (2 * GR)]
                ccwait(v1[ci])
                v2[ci] = coll(mid.ap(), out, g2)
            for ci in cis:
                mid, out = self.agscr[key][ci % (2 * GR)]
                if not NOSCATTER:
                    self.wait_tok("sp", (("c",), v2[ci]))
                    self.agtok[(key, ci % (2 * GR))] = self.dma("sp", dstv[:, ci * rb:(ci + 1) * rb, :],
                                                          out.ap().rearrange("(r k) n -> r k n", k=rb), R=[], W=[])
            ccwait(v2[cis[-1]])
        self.drain_dma(("pool",))

    def flush(self):
        self.drain_dma(("sp", "pool"))
        nc = self.nc
        prog = self.prog
        with nc.Block() as block:
            def mk(ename):
                def body(q):
                    for waits, fn, sk, inc in prog[ename]:
                        for wsk, v in waits:
                            q.wait_ge(self.semobj(wsk), v)
                        if fn is not None:
                            ins = fn(q)
                            ins.then_inc(self.semobj(sk), inc)
                return body
            block.tensor(mk("pe"))
            block.scalar(mk("act"))
            block.vector(mk("dve"))
            block.gpsimd(mk("pool"))
            block.sync(mk("sp"))
        self.reset_stage()
        self.nflush = getattr(self, "nflush", 0) + 1
        if getattr(self, "kstop", 0) and self.nflush >= self.kstop:
            raise StopBuild()


class StopBuild(Exception):
    pass


KSTOP = 0
NOSCATTER = False


def mm(S, out, lhsT, rhs, start=True, stop=True, R=None, W=None):
    return S.op("pe", lambda e: e.matmul(out, lhsT=lhsT, rhs=rhs, start=start, stop=stop),
                R=R if R is not None else [lhsT, rhs], W=W if W is not None else [out])


def act(S, out, in_, func, bias=None, scale=None, R=None, W=None, eng="act"):
    kw = {}
    rr = [in_]
    if bias is not None:
        kw["bias"] = bias
        if not isinstance(bias, (int, float)):
            rr.append(bias)
    if scale is not None:
        kw["scale"] = scale
        if not isinstance(scale, (int, float)):
            rr.append(scale)
    return S.op("act", lambda e: e.activation(out=out, in_=in_, func=func, **kw),
                R=R if R is not None else rr, W=W if W is not None else [out])


def ts(S, out, in0, s1, op0, s2=None, op1=None, eng="dve", R=None, W=None):
    rr = [in0] + [s for s in (s1, s2) if s is not None and not isinstance(s, (int, float))]
    kw = {}
    if op1 is not None:
        kw["op1"] = op1
    return S.op(eng, lambda e: e.tensor_scalar(out=out, in0=in0, scalar1=s1, scalar2=s2, op0=op0, **kw),
                R=R if R is not None else rr, W=W if W is not None else [out])


def tt(S, out, in0, in1, op, eng="dve", R=None, W=None):
    return S.op(eng, lambda e: e.tensor_tensor(out=out, in0=in0, in1=in1, op=op),
                R=R if R is not None else [in0, in1], W=W if W is not None else [out])


def stt(S, out, in0, scalar, in1, op0, op1, R=None, W=None):
    rr = [in0, in1] + ([scalar] if not isinstance(scalar, (int, float)) else [])
    return S.op("dve", lambda e: e.scalar_tensor_tensor(out=out, in0=in0, scalar=scalar, in1=in1, op0=op0, op1=op1),
                R=R if R is not None else rr, W=W if W is not None else [out])


def cp(S, out, in_, eng="dve", R=None, W=None):
    if eng == "act":
        return S.op("act", lambda e: e.activation(out=out, in_=in_, func=AF.Identity),
                    R=R if R is not None else [in_], W=W if W is not None else [out])
    return S.op(eng, lambda e: e.tensor_copy(out=out, in_=in_),
                R=R if R is not None else [in_], W=W if W is not None else [out])


def memset(S, ap, val, eng="dve"):
    return S.op(eng, lambda e: e.memset(ap, val), R=[], W=[ap])


class Ctx:
    def __init__(self, nc, es):
        self.nc = nc
        self.S = Sched(nc, es)
        self.es = None
        self.uid = 0

    def stage(self):
        self.es = ExitStack()
        self._banks = {}
        ctx = self

        class _Stage:
            def __enter__(self_):
                return ctx.es

            def __exit__(self_, et, ev, tb):
                ctx.es.__exit__(None, None, None)
                return False
        return _Stage()

    def bank(self, i):
        if i not in self._banks:
            self._banks[i] = self.ps("bank%d" % i)
        return self._banks[i]

    def sb(self, name, shape, dt):
        self.uid += 1
        return self.es.enter_context(self.nc.sbuf_tensor("%s_%d" % (name, self.uid), list(shape), dt))

    def ps(self, name, shape=(128, 512), dt=F32):
        self.uid += 1
        return self.es.enter_context(self.nc.psum_tensor("%s_%d" % (name, self.uid), list(shape), dt))

    def dram(self, name, shape, dt, ag=False):
        t = self.nc.dram_tensor("D_" + name, list(shape), dt)
        return t


def make_consts(C):
    S = C.S
    k = {}
    ones_f = C.sb("ones_f", [128, 128], F32)
    memset(S, ones_f[:], 1.0)
    ones_b = C.sb("ones_b", [128, 128], BF16)
    memset(S, ones_b[:], 1.0)
    idf = C.sb("idf", [128, 128], F32)
    S.op("pool", lambda e: e.affine_select(out=idf[:], in_=ones_f[:], pattern=[[-1, 128]], compare_op=ALU.is_equal,
                                           fill=0.0, base=0, channel_multiplier=1), R=[ones_f], W=[idf])
    idb = C.sb("idb", [128, 128], BF16)
    cp(S, idb[:], idf[:])
    zeros_f = C.sb("zeros_f", [128, 128], F32)
    memset(S, zeros_f[:], 0.0)
    cbT_f = C.sb("cbT_f", [128, 128], F32)
    S.op("pool", lambda e: e.affine_select(out=cbT_f[:], in_=zeros_f[:], pattern=[[1, 128]], compare_op=ALU.is_ge,
                                           fill=NEG, base=0, channel_multiplier=-1), R=[zeros_f], W=[cbT_f])
    cbT = C.sb("cbT", [128, 128], BF16)
    cp(S, cbT[:], cbT_f[:])
    cbQ_f = C.sb("cbQ_f", [128, 128], F32)
    S.op("pool", lambda e: e.affine_select(out=cbQ_f[:], in_=zeros_f[:], pattern=[[-1, 128]], compare_op=ALU.is_ge,
                                           fill=NEG, base=0, channel_multiplier=1), R=[zeros_f], W=[cbQ_f])
    m01 = C.sb("m01", [128, 128], F32)
    S.op("pool", lambda e: e.affine_select(out=m01[:], in_=ones_f[:], pattern=[[1, 128]], compare_op=ALU.is_ge,
                                           fill=0.0, base=0, channel_multiplier=-1), R=[ones_f], W=[m01])
    epsb = C.sb("epsb", [128, 1], F32)
    memset(S, epsb[:], EPS)
    k.update(epsb=epsb)
    k.update(ones_f=ones_f, ones_b=ones_b, idf=idf, idb=idb, cbT=cbT, cbT_f=cbT_f, cbQ_f=cbQ_f, zeros_f=zeros_f, m01=m01)
    return k


def load_weights(C, wd, K, Mc, name="W"):
    S = C.S
    KC = K // 128
    W = C.sb(name, [128, KC, Mc], BF16)
    stg = [C.sb(name + "stg%d" % i, [128, Mc], F32) for i in range(2)]
    for k in range(KC):
        st = stg[k % 2]
        S.dma("sp", st[:], wd[k * 128:(k + 1) * 128, :])
        cp(S, W[:, k, :], st[:], eng=("dve" if k % 2 == 0 else "pool"), W=[(W.name, k)])
    return W


def dense_stage(C, src, K, T, W, Mc, epilogue, NT=512, prep=None, wkeys=True):
    S = C.S
    KC = K // 128
    MT = (Mc + 127) // 128
    srcv = src.ap().rearrange("(kc p) t -> p kc t", p=128)
    abuf = [C.sb("abuf%d" % i, [128, KC, NT], BF16) for i in range(2)]
    pss = [C.bank(i) for i in range(4)]
    pi = 0
    for j in range(T // NT):
        a = abuf[j % 2]
        h = KC // 2
        S.dma("sp", a[:, 0:h, :], srcv[:, 0:h, j * NT:(j + 1) * NT], W=[(a.name, 0)])
        S.dma("sp", a[:, h:KC, :], srcv[:, h:KC, j * NT:(j + 1) * NT], W=[(a.name, 1)])
        if prep is not None:
            prep(a, j)
        for m in range(MT):
            msz = min(128, Mc - m * 128)
            ps = pss[pi % 4]
            pi += 1
            for k in range(KC):
                mm(S, ps[0:msz, 0:NT], W[:, k, m * 128:m * 128 + msz], a[:, k, :], start=(k == 0), stop=(k == KC - 1),
                   R=[(a.name, 0 if k < h else 1), (W.name, k)], W=[ps])
            epilogue(ps, m, msz, j)


def norm_stage(C, K, xsrc, gain_d, hdst, D, t0, t1, dst_off, NT=256, out_dt=BF16):
    S = C.S
    KC = D // 128
    xv = xsrc.ap().rearrange("(kc p) t -> p kc t", p=128)
    hv = hdst.ap().rearrange("(kc p) t -> p kc t", p=128)
    gain = C.sb("gain", [128, KC], F32)
    S.dma("sp", gain[:], gain_d.ap().rearrange("(kc p) -> p kc", p=128), allow_slow_non_contiguous=True)
    xb = [C.sb("xb%d" % i, [128, KC, NT], F32) for i in range(2)]
    sq = C.sb("sq", [128, KC, NT], BF16)
    hb = [C.sb("hb%d" % i, [128, KC, NT], out_dt) for i in range(2)]
    rstd = C.sb("rstd", [128, NT], F32)
    pss = [C.bank(i) for i in range(2)]
    nt = (t1 - t0) // NT
    h2 = KC // 2
    for j in range(nt):
        x = xb[j % 2]
        c0 = t0 + j * NT
        S.dma("sp", x[:, 0:h2, :], xv[:, 0:h2, c0:c0 + NT], W=[(x.name, 0)])
        S.dma("sp", x[:, h2:KC, :], xv[:, h2:KC, c0:c0 + NT], W=[(x.name, 1)])
        ps = pss[j % 2]
        for k in range(KC):
            act(S, sq[:, k, :], x[:, k, :], AF.Square, R=[(x.name, 0 if k < h2 else 1)], W=[(sq.name, k)])
            mm(S, ps[:, 0:NT], K["ones_b"][:], sq[:, k, :], start=(k == 0), stop=(k == KC - 1),
               R=[(sq.name, k), K["ones_b"]], W=[ps])
        act(S, rstd[:], ps[:, 0:NT], AF.Sqrt, bias=K["epsb"][:, 0:1], scale=1.0 / D)
        S.op("dve", lambda e: e.reciprocal(out=rstd[:], in_=rstd[:]), R=[rstd], W=[rstd])
        hh = hb[j % 2]
        for k in range(KC):
            stt(S, hh[:, k, :], x[:, k, :], gain[:, k:k + 1], rstd[:], ALU.mult, ALU.mult,
                R=[(x.name, 0 if k < h2 else 1), gain, rstd], W=[(hh.name, k)])
        d0 = dst_off + j * NT
        S.dma("pool", hv[:, 0:h2, d0:d0 + NT], hh[:, 0:h2, :], R=[(hh.name, k) for k in range(0, h2)], W=[])
        S.dma("pool", hv[:, h2:KC, d0:d0 + NT], hh[:, h2:KC, :], R=[(hh.name, k) for k in range(h2, KC)], W=[])


def attn_core(C, K, qT, kT, vtok, nblk, bias_fn, out_cb, act_bias=None, pfx="at", pre_tb=None):
    S = C.S
    sps = [C.bank(0), C.bank(1)]
    ops = [C.bank(2), C.bank(3)]
    dps = [C.bank(4), C.bank(5)]
    pT = [C.sb(pfx + "pT%d" % i, [128, 512], BF16) for i in range(3)]
    rden = [C.sb(pfx + "rden%d" % i, [128, 128], F32) for i in range(2)]
    gi = 0
    for tb in range(nblk):
        o_ps = ops[tb % 2]
        d_ps = dps[tb % 2]
        if pre_tb is not None:
            pre_tb(tb)
        for g0 in range(0, tb + 1, 4):
            gs = list(range(g0, min(g0 + 4, tb + 1)))
            sp_ = sps[gi % 2]
            pt = pT[gi % 3]
            gi += 1
            for i, sb in enumerate(gs):
                extra = bias_fn(tb, sb)
                out = sp_[:, i * 128:(i + 1) * 128]
                mm(S, out, kT[:, sb * 128:(sb + 1) * 128], qT[:, tb * 128:(tb + 1) * 128],
                   start=True, stop=(len(extra) == 0), W=[sp_])
                for ei, (l, r, rk) in enumerate(extra):
                    mm(S, out, l, r, start=False, stop=(ei == len(extra) - 1), R=rk, W=[sp_])
            n = len(gs) * 128
            bkw = {} if act_bias is None else {"bias": act_bias(tb)}
            act(S, pt[:, 0:n], sp_[:, 0:n], AF.Exp, **bkw)
            for i, sb in enumerate(gs):
                mm(S, o_ps[:, 0:128], vtok[:, sb, :], pt[:, i * 128:(i + 1) * 128], start=(sb == 0), stop=(sb == tb),
                   R=[vtok, pt], W=[o_ps])
                mm(S, d_ps[:, 0:128], K["ones_b"][:], pt[:, i * 128:(i + 1) * 128], start=(sb == 0), stop=(sb == tb),
                   R=[pt], W=[d_ps])
        rd = rden[tb % 2]
        S.op("dve", lambda e, rd=rd, d_ps=d_ps: e.reciprocal(out=rd[:], in_=d_ps[:, 0:128]), R=[d_ps], W=[rd])
        out_cb(tb, o_ps, rd)


def to_tokmajor(C, K, srcT, nblk, name):
    S = C.S
    dst = C.sb(name, [128, nblk, 128], BF16)
    pss = [C.bank(0), C.bank(1)]
    for g in range(0, nblk, 4):
        ps = pss[(g // 4) % 2]
        n = min(4, nblk - g)
        for i in range(n):
            mm(S, ps[:, i * 128:(i + 1) * 128], srcT[:, (g + i) * 128:(g + i + 1) * 128], K["idb"][:], W=[ps])
        cp(S, dst[:, g:g + n, :], ps[:, 0:n * 128].rearrange("p (a b) -> p a b", b=128), eng="act", W=[(dst.name, g)])
    return dst


def fox_head(C, K, PJ, rq, rk, rv, rf, fbias_d, ydst, yrow0, B, SEQ):
    S = C.S
    nblk = SEQ // 128
    PJa = PJ.ap()
    stg = C.sb("fx_stg", [128, SEQ], F32)
    qT = C.sb("fx_qT", [128, SEQ], BF16)
    kT = C.sb("fx_kT", [128, SEQ], BF16)
    vT = C.sb("fx_vT", [128, SEQ], BF16)
    frow = C.sb("fx_frow", [1, SEQ], F32)
    erow = C.sb("fx_erow", [1, SEQ], F32)
    lnrow = C.sb("fx_lnrow", [1, SEQ], F32)
    ncum = C.sb("fx_ncum", [1, SEQ], F32)
    onesrow = C.sb("fx_ones", [1, SEQ], F32)
    fb = C.sb("fx_fb", [1, 1], F32)
    nfb = C.sb("fx_nfb", [1, 1], F32)
    cmidbc = C.sb("fx_cmid", [128, nblk], F32)
    cps = C.bank(6)
    ob = [C.sb("fx_ob%d" % i, [128, 128], BF16) for i in range(2)]
    memset(S, onesrow[:], 1.0)
    S.dma("sp", fb[:], fbias_d)
    ts(S, nfb[:], fb[:], -1.0, ALU.mult)
    for b in range(B):
        t0 = b * SEQ
        S.dma("sp", stg[:], PJa[rq:rq + 128, t0:t0 + SEQ])
        ts(S, qT[:], stg[:], 128 ** -0.5, ALU.mult)
        S.dma("sp", stg[:], PJa[rk:rk + 128, t0:t0 + SEQ])
        cp(S, kT[:], stg[:], eng="pool")
        S.dma("sp", stg[:], PJa[rv:rv + 128, t0:t0 + SEQ])
        cp(S, vT[:], stg[:], eng="dve")
        vtok = to_tokmajor(C, K, vT, nblk, "fx_vtok%d" % b)
        S.dma("sp", frow[:], PJa[rf:rf + 1, t0:t0 + SEQ])
        act(S, erow[:], frow[:], AF.Exp, bias=nfb[:, 0:1], scale=-1.0)
        act(S, lnrow[:], erow[:], AF.Ln, bias=1.0)
        S.op("dve", lambda e: e.tensor_tensor_scan(out=ncum[:], data0=onesrow[:], data1=lnrow[:], initial=0.0,
                                                   op0=ALU.mult, op1=ALU.add), R=[onesrow, lnrow], W=[ncum])
        mids = ncum[0:1, :].rearrange("p (a b) -> p a b", b=128)[:, :, 64:65].rearrange("p a b -> p (a b)")
        mm(S, cps[:, 0:nblk], K["ones_f"][0:1, :], mids, R=[ncum], W=[cps])
        ts(S, cmidbc[:], cps[:, 0:nblk], -1.0, ALU.mult)

        def bias_fn(tb, sb):
            ex = [(ncum[0:1, sb * 128:(sb + 1) * 128], K["ones_f"][0:1, :], [ncum])]
            if sb == tb:
                ex.append((K["idb"][:], K["cbT"][:], []))
            return ex

        def out_cb(tb, o_ps, rd, t0=t0):
            o = ob[tb % 2]
            tt(S, o[:], o_ps[:, 0:128], rd[:], ALU.mult)
            S.dma("pool", ydst.ap()[yrow0:yrow0 + 128, t0 + tb * 128:t0 + (tb + 1) * 128], o[:])

        attn_core(C, K, qT, kT, vtok, nblk, bias_fn, out_cb, act_bias=lambda tb: cmidbc[:, tb:tb + 1], pfx="fx%d" % b)


I32 = mybir.dt.int32


def dyn_dma(S, out, tensor, tbl_ap, pattern, R, W):
    if not hasattr(S, "dynregs"):
        S.dynregs = [S.es_outer.enter_context(S.nc.gpsimd.register("dynr%d" % i)) for i in range(8)]
        S.dyni = 0
    gr = S.dynregs[S.dyni % 8]
    S.dyni += 1

    def fn(e):
        e.reg_load(gr, tbl_ap)
        return e.dma_start(out=out, in_=bass.AP(tensor, gr, pattern))
    Rk = [S.key(x) for x in R]
    Wk = [S.key(x) for x in W]
    waits = S._deps("pool", Rk, Wk)
    half = S.NDS // 2
    slot = half + S.dnext["pool"]
    S.dnext["pool"] = (S.dnext["pool"] + 1) % half
    prev = S.dcnt[slot]
    sk = ("d", slot)
    if prev > 0 and S.waited["pool"].get(sk, 0) < prev:
        S.waited["pool"][sk] = prev
        waits.append((sk, prev))
    S.dcnt[slot] += 16
    tok = (sk, S.dcnt[slot])
    S.prog["pool"].append((waits, fn, sk, 16))
    S._commit(tok, Rk, Wk)


def norm_tile(C, K, x, KC, n, gain, o, bank, nfeat, scratch, okey=None):
    S = C.S
    sq, rstd = scratch
    for k in range(KC):
        act(S, sq[:, k, :], x[:, k, :], AF.Square, R=[x], W=[(sq.name, k)])
        mm(S, bank[:, 0:n], K["ones_b"][:], sq[:, k, :], start=(k == 0), stop=(k == KC - 1), R=[(sq.name, k)], W=[bank])
    act(S, rstd[:], bank[:, 0:n], AF.Sqrt, bias=K["epsb"][:, 0:1], scale=1.0 / nfeat)
    S.op("dve", lambda e: e.reciprocal(out=rstd[:], in_=rstd[:]), R=[rstd], W=[rstd])
    for k in range(KC):
        stt(S, o[:, k, :], x[:, k, :], gain[:, k:k + 1], rstd[:], ALU.mult, ALU.mult, R=[x, gain, rstd],
            W=[okey if okey is not None else o])


def dsa_index_stage(C, K, SH, B, SEQ, topk, w_iq_d, qn_d, tbl_d, ixmask_d, MBloc, dbg=None):
    S = C.S
    T = B * SEQ
    nblk = SEQ // 128
    nslot = nblk // 4
    Wiq = load_weights(C, w_iq_d, 768, 1024, name="Wiq")
    tbl = C.sb("ix_tbl", [1, 16], I32)
    S.dma("sp", tbl[:], tbl_d.ap()[:, :])
    ixm = C.sb("ix_mask", [128, 512], F32)
    S.dma("sp", ixm[:], ixmask_d.ap()[:, :])
    ixmb = C.sb("ix_maskb", [128, 512], BF16)
    cp(S, ixmb[:], ixm[:])
    gain = C.sb("ix_gain", [128, 6], F32)
    S.dma("sp", gain[:], qn_d.ap().rearrange("(kc p) -> p kc", p=128), allow_slow_non_contiguous=True)
    kstg = C.sb("kstg", [128, SEQ], F32)
    kidx2 = C.sb("kidx2", [128, SEQ], BF16)
    dyn_dma(S, kstg[0:64, :], SH, tbl[0:1, 0:1], [[T, 64], [1, SEQ]], R=[tbl], W=[kstg])
    S.dma("sp", kstg[64:128, :], kstg[0:64, :])
    cp(S, kidx2[:], kstg[:])
    acc = C.sb("ix_acc", [128, SEQ], F32)
    work = [C.sb("ix_work%d" % i, [128, SEQ], F32) for i in range(2)]
    tmp = [C.sb("ix_tmp%d" % i, [128, 512], F32) for i in range(3)]
    cqx_all = C.sb("ix_cqx", [128, 6, nslot, 128], F32)
    widx_all = C.sb("ix_widxT", [16, nslot, 128], F32)
    for kc in range(6):
        dyn_dma(S, cqx_all[:, kc, :, :], SH, tbl[0:1, 1 + kc:2 + kc], [[T, 128], [512, nslot], [1, 128]], R=[tbl], W=[cqx_all])
    dyn_dma(S, widx_all[:], SH, tbl[0:1, 7:8], [[T, 16], [512, nslot], [1, 128]], R=[tbl], W=[widx_all])
    cqn = C.sb("ix_cqn", [128, 6, 128], BF16)
    qiT = C.sb("qiT", [128, 8, 128], BF16)
    wt = C.sb("ix_wt", [128, 16], F32)
    wabs = C.sb("ix_wabs", [128, 16], F32)
    wsgn = C.sb("ix_wsgn", [128, 16], F32)
    m8 = C.sb("ix_m8", [128, 8], F32)
    MBt = [C.sb("ix_MBt%d" % i, [128, SEQ], BF16) for i in range(2)]
    ti = 0
    nsc = (C.sb("ix_nsq", [128, 6, 128], BF16), C.sb("ix_nrstd", [128, 128], F32))
    for i in range(nslot):
        n = (4 * i + 4) * 128
        cx = cqx_all[:, :, i, :]
        wx = widx_all[:, i, :]
        norm_tile(C, K, cx, 6, 128, gain, cqn, C.bank(7), 768, nsc)
        for half in range(2):
            ps = C.bank(half)
            for pp in range(4):
                p = half * 4 + pp
                for k in range(6):
                    mm(S, ps[:, pp * 128:(pp + 1) * 128], Wiq[:, k, p * 128:(p + 1) * 128], cqn[:, k, :],
                       start=(k == 0), stop=(k == 5), R=[cqn, (Wiq.name, k)], W=[ps])
            cp(S, qiT[:, half * 4:(half + 1) * 4, :], ps[:, 0:512].rearrange("p (a b) -> p a b", b=128), eng="act", W=[qiT])
        ps = C.bank(2)
        mm(S, ps[:, 0:16], wx, K["idf"][0:16, 0:16], W=[ps])
        ts(S, wt[:], ps[:, 0:16], 1.0 / 32.0, ALU.mult)
        act(S, wabs[:], wt[:], AF.Abs)
        act(S, wsgn[:], wt[:], AF.Sign)
        nkc = n // 512
        for kc in range(nkc):
            for h in range(16):
                p, half = h // 2, h % 2
                ps = C.bank(3 + (ti % 4))
                tm = tmp[ti % 3]
                ti += 1
                mm(S, ps[:, 0:512], qiT[half * 64:(half + 1) * 64, p, :], kidx2[half * 64:(half + 1) * 64, kc * 512:(kc + 1) * 512],
                   R=[qiT, kidx2], W=[ps])
                act(S, tm[:], ps[:, 0:512], AF.Relu, scale=wabs[:, h:h + 1])
                a = acc[:, kc * 512:(kc + 1) * 512]
                if h == 0:
                    ts(S, a, tm[:], wsgn[:, h:h + 1], ALU.mult, W=[(acc.name, kc)])
                else:
                    stt(S, a, tm[:], wsgn[:, h:h + 1], a, ALU.mult, ALU.add, R=[tm, wsgn, (acc.name, kc)], W=[(acc.name, kc)])
        allk = [(acc.name, kc) for kc in range(nkc)]
        tail = acc[:, n - 512:n]
        tt(S, tail, tail, ixm[:], ALU.add, R=allk + [ixm], W=allk)
        mb = MBt[i % 2]
        rounds = topk // 8
        src = acc
        spc = C.sb("ix_spc", [128, 256], F32)

        def spacer():
            S.op("dve", lambda e: e.tensor_copy(out=spc[:], in_=ixm[:, 0:256]), R=[], W=[])
        spacer()
        for r in range(rounds):
            S.op("dve", lambda e, src=src, n=n: e.max(out=m8[:], in_=src[:, 0:n]), R=(allk if src is acc else [src]), W=[m8])
            spacer()
            if r < rounds - 1:
                dst = work[r % 2]
                S.op("dve", lambda e, src=src, dst=dst, n=n: e.match_replace(out=dst[:, 0:n], in_to_replace=m8[:], in_values=src[:, 0:n],
                                                                       imm_value=-1.0e30),
                     R=(allk if src is acc else [src]) + [m8], W=[dst])
                spacer()
                src = dst
        ts(S, work[0][:, 0:n], acc[:, 0:n], m8[:, 7:8], ALU.is_lt, R=allk + [m8], W=[work[0]])
        ts(S, mb[:, 0:n], work[0][:, 0:n], NEG, ALU.mult, R=[work[0]], W=[mb])
        tt(S, mb[:, n - 512:n], mb[:, n - 512:n], ixmb[:], ALU.min)
        S.dma("sp", MBloc.ap()[i * 128:(i + 1) * 128, 0:n], mb[:, 0:n])
        if dbg is not None:
            S.dma("pool", dbg.ap()[i * 128:(i + 1) * 128, 0:n], mb[:, 0:n])
            S.dma("sp", dbg.ap()[(nslot + i) * 128:(nslot + i + 1) * 128, 0:n], acc[:, 0:n], R=allk, W=[])


def dsa_attn_stage(C, K, B, SEQ, SH, MBall, wuq_d, wuk_d, wuv_d, qn_d, kvn_d, ydst, yrow0):
    S = C.S
    nblk = SEQ // 128
    nslot = nblk // 4
    NT = 512
    Wuq = load_weights(C, wuq_d, 768, 128, name="Wuq")
    Wuk = load_weights(C, wuk_d, 512, 128, name="Wuk")
    Wuv = load_weights(C, wuv_d, 512, 128, name="Wuv")
    gq = C.sb("ds_gq", [128, 6], F32)
    S.dma("sp", gq[:], qn_d.ap().rearrange("(kc p) -> p kc", p=128), allow_slow_non_contiguous=True)
    gkv = C.sb("ds_gkv", [128, 4], F32)
    S.dma("sp", gkv[:], kvn_d.ap().rearrange("(kc p) -> p kc", p=128), allow_slow_non_contiguous=True)
    xq = [C.sb("ds_xq%d" % i, [128, 6, NT], F32) for i in range(2)]
    xkv = [C.sb("ds_xkv%d" % i, [128, 4, NT], F32) for i in range(2)]
    lq = C.sb("ds_lq", [128, 6, NT], BF16)
    lkv = C.sb("ds_lkv", [128, 4, NT], BF16)
    qT = C.sb("ds_qT", [128, SEQ], BF16)
    kT = C.sb("ds_kT", [128, SEQ], BF16)
    vtok = C.sb("ds_vtok", [128, nblk, 128], BF16)
    MBt = [C.sb("ds_MBt%d" % i, [128, SEQ], BF16) for i in range(2)]
    ob = [C.sb("ds_ob%d" % i, [128, 128], BF16) for i in range(2)]
    shv = SH.ap()
    nsc1 = (C.sb("ds_nsq1", [128, 6, NT], BF16), C.sb("ds_nrstd1", [128, NT], F32))
    nsc2 = (C.sb("ds_nsq2", [128, 4, NT], BF16), C.sb("ds_nrstd2", [128, NT], F32))
    for b in range(B):
        t0 = b * SEQ
        for j in range(SEQ // NT):
            c0 = t0 + j * NT
            x1 = xq[j % 2]
            x2 = xkv[j % 2]
            S.dma("sp", x1[:], shv[0:768, c0:c0 + NT].rearrange("(kc p) t -> p kc t", p=128))
            S.dma("sp", x2[:], shv[768:1280, c0:c0 + NT].rearrange("(kc p) t -> p kc t", p=128))
            norm_tile(C, K, x1, 6, NT, gq, lq, C.bank(6), 768, nsc1)
            norm_tile(C, K, x2, 4, NT, gkv, lkv, C.bank(7), 512, nsc2)
            ps = C.bank(j % 2)
            for k in range(6):
                mm(S, ps[:, 0:NT], Wuq[:, k, :], lq[:, k, :], start=(k == 0), stop=(k == 5), R=[lq, (Wuq.name, k)], W=[ps])
            ts(S, qT[:, j * NT:(j + 1) * NT], ps[:, 0:NT], 128 ** -0.5, ALU.mult)
            ps = C.bank(2 + j % 2)
            for k in range(4):
                mm(S, ps[:, 0:NT], Wuk[:, k, :], lkv[:, k, :], start=(k == 0), stop=(k == 3), R=[lkv, (Wuk.name, k)], W=[ps])
            cp(S, kT[:, j * NT:(j + 1) * NT], ps[:, 0:NT], eng="act")
            ps = C.bank(4 + j % 2)
            for i in range(4):
                for k in range(4):
                    mm(S, ps[:, i * 128:(i + 1) * 128], lkv[:, k, i * 128:(i + 1) * 128], Wuv[:, k, :],
                       start=(k == 0), stop=(k == 3), R=[lkv, (Wuv.name, k)], W=[ps])
            cp(S, vtok[:, j * 4:(j + 1) * 4, :], ps[:, 0:512].rearrange("p (a b) -> p a b", b=128), eng="act", W=[vtok])

        def pre_tb(tb, b=b):
            r = (2 * (tb % 4) + b) * nslot + tb // 4
            S.dma("sp", MBt[tb % 2][:, 0:(tb + 1) * 128], MBall.ap()[r * 128:(r + 1) * 128, 0:(tb + 1) * 128])

        def bias_fn(tb, sb):
            return [(MBt[tb % 2][:, sb * 128:(sb + 1) * 128], K["idb"][:], [MBt[tb % 2]])]

        def out_cb(tb, o_ps, rd, t0=t0):
            o = ob[tb % 2]
            tt(S, o[:], o_ps[:, 0:128], rd[:], ALU.mult)
            S.dma("pool", ydst.ap()[yrow0:yrow0 + 128, t0 + tb * 128:t0 + (tb + 1) * 128], o[:])

        attn_core(C, K, qT, kT, vtok, nblk, bias_fn, out_cb, pfx="ds%d" % b, pre_tb=pre_tb)


def dsa_host_tables(c, B, SEQ):
    T = B * SEQ
    nslot = (SEQ // 128) // 4
    b, d = c % 2, c // 2
    tbl = np.zeros((1, 16), np.int32)
    tbl[0, 0] = 1280 * T + b * SEQ
    for kc in range(6):
        tbl[0, 1 + kc] = kc * 128 * T + b * SEQ + d * 128
    tbl[0, 7] = 1344 * T + b * SEQ + d * 128
    r = np.arange(128)[:, None]
    j = np.arange(512)[None, :]
    ixmask = np.where(j <= d * 128 + r, 0.0, NEG).astype(np.float32)
    return tbl, ixmask


def mamba_stage(C, K, P, rows, B, SEQ, cw_d, cb_d, dtb_d, alog_d, dsk_d, ydst, yrow0, SSloc):
    S = C.S
    Pa = P.ap()
    nch = SEQ // 128
    cw = C.sb("mb_cw", [128, 12], F32)
    S.dma("sp", cw[:], cw_d.ap()[:, :])
    cb = C.sb("mb_cb", [128, 3], F32)
    S.dma("sp", cb[:], cb_d.ap()[:, :])
    dsk = C.sb("mb_dsk", [128, 1], F32)
    S.dma("sp", dsk[:], dsk_d.ap()[:, :])
    dtb = C.sb("mb_dtb", [33, 1], F32)
    nA = C.sb("mb_nA", [33, 1], F32)
    memset(S, dtb[:], 0.0)
    memset(S, nA[:], 0.0)
    for hh in range(2):
        S.dma("sp", dtb[32 * hh:32 * hh + 1, :], dtb_d.ap()[0:1, hh:hh + 1])
        S.dma("sp", nA[32 * hh:32 * hh + 1, :], alog_d.ap()[0:1, hh:hh + 1])
    act(S, nA[:], nA[:], AF.Exp)
    ts(S, nA[:], nA[:], -1.0, ALU.mult)
    raw = C.sb("mb_raw", [128, 3 + SEQ], F32)
    cacc = C.sb("mb_cacc", [128, SEQ], F32)
    xsb = C.sb("mb_xsb", [128, SEQ], BF16)
    Bc = C.sb("mb_Bc", [128, SEQ], BF16)
    Cc = C.sb("mb_Cc", [128, SEQ], BF16)
    Ct = [C.sb("mb_Ct%d" % h, [128, SEQ], BF16) for h in range(2)]
    rA = C.sb("mb_rA", [33, SEQ], F32)
    rB = C.sb("mb_rB", [33, SEQ], F32)
    rC = C.sb("mb_rC", [33, SEQ], F32)
    rD = C.sb("mb_rD", [33, SEQ], F32)
    rE = C.sb("mb_rE", [33, SEQ], BF16)
    ecl = C.sb("mb_ecl", [128, 2, nch], F32)
    wtok = C.sb("mb_wtok", [128, 2 * nch], F32)
    stF = C.sb("mb_stF", [128, 128], F32)
    stT = C.sb("mb_stT", [128, 128], BF16)
    LT = [C.sb("mb_LT%d" % i, [128, 256], F32) for i in range(2)]
    MT = [C.sb("mb_MT%d" % i, [128, 256], BF16) for i in range(2)]
    Bw = [C.sb("mb_Bw%d" % i, [128, 256], BF16) for i in range(2)]
    zr = [C.sb("mb_zr%d" % i, [128, 512], F32) for i in range(2)]
    zs = [C.sb("mb_zs%d" % i, [128, 512], F32) for i in range(2)]
    y1 = [C.sb("mb_y1%d" % i, [128, 128], F32) for i in range(2)]
    ysq = [C.sb("mb_ysq%d" % i, [128, 128], BF16) for i in range(2)]
    yo = [C.sb("mb_yo%d" % i, [128, 128], BF16) for i in range(2)]
    ssrow = [C.sb("mb_ssrow%d" % i, [1, 512], F32) for i in range(2)]
    memset(S, raw[:, 0:3], 0.0)
    memset(S, rE[:], 1.0)
    memset(S, rE[:, :].rearrange("p (c l) -> p c l", l=128)[:, :, 0:1], 0.0)
    for b in range(B):
        t0 = b * SEQ
        memset(S, raw[:, 0:3], 0.0)
        for g, (r0, dst) in enumerate(((rows["x"], xsb), (rows["Bm"], Bc), (rows["Cm"], Cc))):
            S.dma("sp", raw[:, 3:3 + SEQ], Pa[r0:r0 + 128, t0:t0 + SEQ])
            ts(S, cacc[:], raw[:, 0:SEQ], cw[:, 4 * g:4 * g + 1], ALU.mult, cb[:, g:g + 1], ALU.add)
            for k in range(1, 4):
                stt(S, cacc[:], raw[:, k:k + SEQ], cw[:, 4 * g + k:4 * g + k + 1], cacc[:], ALU.mult, ALU.add)
            act(S, dst[:], cacc[:], AF.Silu)
        memset(S, rA[:], 0.0)
        for hh in range(2):
            S.dma("sp", rA[32 * hh:32 * hh + 1, :], Pa[rows["dt"] + hh:rows["dt"] + hh + 1, t0:t0 + SEQ])
        act(S, rA[:], rA[:], AF.Exp, bias=dtb[:, 0:1])
        act(S, rA[:], rA[:], AF.Ln, bias=1.0)
        act(S, rB[:], rA[:], AF.Ln)
        ts(S, rC[:], rA[:], nA[:, 0:1], ALU.mult)
        S.op("dve", lambda e: e.tensor_tensor_scan(out=rD[:], data0=rE[:], data1=rC[:], initial=0.0,
                                                   op0=ALU.mult, op1=ALU.add), R=[rE, rC], W=[rD])
        tt(S, rB[:], rB[:], rD[:], ALU.subtract)
        act(S, rA[:], rD[:], AF.Exp)
        last = rD[:, :].rearrange("p (c l) -> p c l", l=128)[:, :, 127:128].broadcast_to([33, nch, 128])
        tt(S, rC[:, :].rearrange("p (c l) -> p c l", l=128), rB[:, :].rearrange("p (c l) -> p c l", l=128), last, ALU.add,
           R=[rB, rD], W=[rC])
        act(S, rC[:], rC[:], AF.Exp)
        Ebc = raw
        for hh in range(2):
            p0 = 32 * hh
            for j in range(SEQ // 512):
                ps = C.bank(j % 2)
                mm(S, ps[:, 0:512], K["ones_f"][p0:p0 + 1, 0:128], rA[p0:p0 + 1, j * 512:(j + 1) * 512], R=[rA], W=[ps])
                cp(S, Ebc[:, j * 512:(j + 1) * 512], ps[:, 0:512], eng="act", W=[Ebc])
            tt(S, Ct[hh][:], Cc[:], Ebc[:, 0:SEQ], ALU.mult)
            cp(S, ecl[:, hh, :], Ebc[:, 0:SEQ].rearrange("p (c l) -> p c l", l=128)[:, :, 127:128].rearrange("p c l -> p (c l)"),
               R=[Ebc], W=[ecl])
            ps = C.bank(0)
            for j in range(nch):
                mm(S, ps[:, hh * nch + j:hh * nch + j + 1], rC[p0:p0 + 1, j * 128:(j + 1) * 128], K["ones_f"][p0:p0 + 1, 0:1],
                   R=[rC], W=[ps])
            cp(S, wtok[:, hh * nch:(hh + 1) * nch], ps[:, hh * nch:(hh + 1) * nch])
        xtok = to_tokmajor(C, K, xsb, nch, "mb_xtok%d" % b)
        Btok = to_tokmajor(C, K, Bc, nch, "mb_Btok%d" % b)
        memset(S, stF[:], 0.0)
        memset(S, stT[:], 0.0)
        ps_ss = C.bank(7)
        for j in range(nch):
            ck = slice(j * 128, (j + 1) * 128)
            if j % 4 == 0:
                zz = zr[(j // 4) % 2]
                S.dma("sp", zz[:], Pa[rows["z"]:rows["z"] + 128, t0 + j * 128:t0 + j * 128 + 512])
                zq = zs[(j // 4) % 2]
                act(S, zq[:], zz[:], AF.Silu)
            ps_g = C.bank(2)
            ps_d = C.bank(3)
            for hh in range(2):
                p0 = 32 * hh
                mm(S, ps_g[:, hh * 128:(hh + 1) * 128], Bc[:, ck], Cc[:, ck], W=[ps_g])
                o_ = ps_d[:, hh * 128:(hh + 1) * 128]
                mm(S, o_, K["ones_f"][p0:p0 + 1, 0:128], rD[p0:p0 + 1, ck], start=True, stop=False, R=[rD], W=[ps_d])
                mm(S, o_, rB[p0:p0 + 1, ck], K["ones_f"][p0:p0 + 1, 0:128], start=False, stop=False, R=[rB], W=[ps_d])
                mm(S, o_, K["idb"][:], K["cbT"][:], start=False, stop=True, R=[], W=[ps_d])
            lt = LT[j % 2]
            act(S, lt[:], ps_d[:, 0:256], AF.Exp)
            mt = MT[j % 2]
            tt(S, mt[:], ps_g[:, 0:256], lt[:], ALU.mult)
            ps_y = C.bank(4 + j % 2)
            for hh in range(2):
                hs = slice(hh * 64, (hh + 1) * 64)
                mm(S, ps_y[hs, 0:128], xtok[:, j, hs], mt[:, hh * 128:(hh + 1) * 128], start=True, stop=(j == 0),
                   R=[xtok, mt], W=[ps_y])
                if j > 0:
                    mm(S, ps_y[hs, 0:128], stT[:, hs], Ct[hh][:, ck], start=False, stop=True, R=[stT, Ct[hh]], W=[ps_y])
            bw = Bw[j % 2]
            ps_s = C.bank(6)
            for hh in range(2):
                hs = slice(hh * 64, (hh + 1) * 64)
                ts(S, bw[:, hh * 128:(hh + 1) * 128], Btok[:, j, :], wtok[:, hh * nch + j:hh * nch + j + 1], ALU.mult,
                   R=[Btok, wtok], W=[bw])
                mm(S, ps_s[:, hs], bw[:, hh * 128:(hh + 1) * 128], xtok[:, j, hs], R=[bw, xtok], W=[ps_s])
            for hh in range(2):
                hs = slice(hh * 64, (hh + 1) * 64)
                stt(S, stF[:, hs], stF[:, hs], ecl[:, hh, j:j + 1], ps_s[:, hs], ALU.mult, ALU.add, R=[stF, ecl, ps_s], W=[stF])
            cp(S, stT[:], stF[:])
            a1 = y1[j % 2]
            stt(S, a1[:], xsb[:, ck], dsk[:, 0:1], ps_y[:, 0:128], ALU.mult, ALU.add, R=[xsb, dsk, ps_y], W=[a1])
            tt(S, a1[:], a1[:], zs[(j // 4) % 2][:, (j % 4) * 128:(j % 4 + 1) * 128], ALU.mult)
            sq_ = ysq[j % 2]
            act(S, sq_[:], a1[:], AF.Square)
            mm(S, ps_ss[:, (j % 4) * 128:(j % 4 + 1) * 128], K["ones_b"][:], sq_[:], R=[sq_], W=[ps_ss])
            o = yo[j % 2]
            cp(S, o[:], a1[:], eng="pool")
            S.dma("pool", ydst.ap()[yrow0:yrow0 + 128, t0 + j * 128:t0 + (j + 1) * 128], o[:])
            if j % 4 == 3:
                sr = ssrow[(j // 4) % 2]
                cp(S, sr[:], ps_ss[0:1, 0:512])
                S.dma("pool", SSloc.ap()[0:1, t0 + (j - 3) * 128:t0 + (j + 1) * 128], sr[:])


def hgrn_stage(C, K, P, rows, B, SEQ, layer, lbl_d, gain_d, ydst, yrow0):
    S = C.S
    Pa = P.ap()
    CH = 32
    nch = SEQ // CH
    lbl = C.sb("hg_lbl", [128, 2], F32)
    S.dma("sp", lbl[:], lbl_d.ap()[:, :])
    gain = C.sb("hg_gain", [128, 1], F32)
    S.dma("sp", gain[:], gain_d.ap()[:, :])
    lb = C.sb("hg_lb", [128, 1], F32)
    oml = C.sb("hg_oml", [128, 1], F32)
    if layer == 0:
        memset(S, lb[:], 0.0)
    else:
        tt(S, lb[:], lbl[:, 1:2], lbl[:, 0:1], ALU.subtract)
        act(S, lb[:], lb[:], AF.Sigmoid)
    ts(S, oml[:], lb[:], -1.0, ALU.mult, 1.0, ALU.add)
    tq = C.sb("hg_tq", [128, SEQ], F32)
    tf = C.sb("hg_tf", [128, SEQ], F32)
    tk = C.sb("hg_tk", [128, SEQ], F32)
    tb = C.sb("hg_tb", [128, SEQ], F32)
    td = C.sb("hg_td", [128, SEQ], F32)
    msk = C.sb("hg_msk", [128, SEQ], BF16)
    qt = C.sb("hg_qt", [128, SEQ], BF16)
    kt = C.sb("hg_kt", [128, SEQ], BF16)
    qin = C.sb("hg_qin", [128, SEQ], BF16)
    vT = C.sb("hg_vT", [128, SEQ], BF16)
    ebl = C.sb("hg_ebl", [128, nch], F32)
    SF = C.sb("hg_SF", [128, 128], F32)
    Sb = C.sb("hg_Sb", [128, 128], BF16)
    kvt = [C.sb("hg_kvt%d" % i, [32, 512], BF16) for i in range(2)]
    AT = [C.sb("hg_AT%d" % i, [32, 32], BF16) for i in range(3)]
    gz = [C.sb("hg_gz%d" % i, [128, 512], F32) for i in range(2)]
    og = [C.sb("hg_og%d" % i, [128, 512], F32) for i in range(2)]
    osq = C.sb("hg_osq", [128, 512], BF16)
    rstd = C.sb("hg_rstd", [128, 512], F32)
    yo = [C.sb("hg_yo%d" % i, [128, 512], BF16) for i in range(2)]
    memset(S, msk[:], 1.0)
    memset(S, msk[:, :].rearrange("p (c l) -> p c l", l=CH)[:, :, 0:1], 0.0)
    v3 = lambda t: t[:, :].rearrange("p (c l) -> p c l", l=CH)
    for b in range(B):
        t0 = b * SEQ
        S.dma("sp", tq[:], Pa[rows["q"]:rows["q"] + 128, t0:t0 + SEQ])
        act(S, tq[:], tq[:], AF.Silu)
        S.dma("sp", tf[:], Pa[rows["f"]:rows["f"] + 128, t0:t0 + SEQ])
        act(S, tf[:], tf[:], AF.Sigmoid)
        ts(S, tf[:], tf[:], oml[:, 0:1], ALU.mult, lb[:, 0:1], ALU.add)
        ts(S, tk[:], tf[:], -1.0, ALU.mult, 1.0, ALU.add)
        act(S, tf[:], tf[:], AF.Ln)
        S.op("dve", lambda e: e.tensor_tensor_scan(out=tb[:], data0=msk[:], data1=tf[:], initial=0.0,
                                                   op0=ALU.mult, op1=ALU.add), R=[msk, tf], W=[tb])
        tt(S, v3(td), v3(tb), v3(tb)[:, :, CH - 1:CH].broadcast_to([128, nch, CH]), ALU.subtract, R=[tb], W=[td])
        act(S, tf[:], td[:], AF.Exp, scale=-1.0)
        act(S, td[:], td[:], AF.Exp)
        act(S, tb[:], tb[:], AF.Exp)
        tt(S, qt[:], tq[:], td[:], ALU.mult)
        tt(S, kt[:], tk[:], tf[:], ALU.mult)
        tt(S, qin[:], tq[:], tb[:], ALU.mult)
        cp(S, ebl[:], v3(tb)[:, :, CH - 1:CH].rearrange("p c l -> p (c l)"), R=[tb], W=[ebl])
        S.dma("sp", tk[:], Pa[rows["i"]:rows["i"] + 128, t0:t0 + SEQ])
        act(S, vT[:], tk[:], AF.Silu)
        memset(S, SF[:], 0.0)
        memset(S, Sb[:], 0.0)
        for j in range(nch):
            ck = slice(j * CH, (j + 1) * CH)
            if j % 2 == 0:
                ps_t = C.bank((j // 2) % 2)
                for jj in range(2):
                    c2 = slice((j + jj) * CH, (j + jj + 1) * CH)
                    mm(S, ps_t[0:32, jj * 256:jj * 256 + 128], kt[:, c2], K["idb"][:], R=[kt], W=[ps_t])
                    mm(S, ps_t[0:32, jj * 256 + 128:jj * 256 + 256], vT[:, c2], K["idb"][:], R=[vT], W=[ps_t])
                kv = kvt[(j // 2) % 2]
                cp(S, kv[:], ps_t[0:32, 0:512], eng="act")
            kv = kvt[(j // 2) % 2]
            ktok = kv[:, (j % 2) * 256:(j % 2) * 256 + 128]
            vtok = kv[:, (j % 2) * 256 + 128:(j % 2) * 256 + 256]
            ps_sc = C.bank(2 + j % 2)
            mm(S, ps_sc[0:32, 0:32], kt[:, ck], qt[:, ck], R=[kt, qt], W=[ps_sc])
            at = AT[j % 3]
            tt(S, at[:], ps_sc[0:32, 0:32], K["m01"][0:32, 0:32], ALU.mult)
            if j % 16 == 0:
                ps_o = C.bank(4 + (j // 16) % 2)
            oc = ps_o[:, (j % 16) * CH:(j % 16 + 1) * CH]
            mm(S, oc, vtok, at[:], start=True, stop=(j == 0), R=[kv, at], W=[ps_o])
            if j > 0:
                mm(S, oc, Sb[:], qin[:, ck], start=False, stop=True, R=[Sb, qin], W=[ps_o])
            ps_s = C.bank(6)
            mm(S, ps_s[:, 0:128], ktok, vtok, R=[kv], W=[ps_s])
            stt(S, SF[:], SF[:], ebl[:, j:j + 1], ps_s[:, 0:128], ALU.mult, ALU.add)
            cp(S, Sb[:], SF[:])
            if j % 16 == 15:
                g = j // 16
                c0 = t0 + g * 512
                gzz = gz[g % 2]
                S.dma("sp", gzz[:], Pa[rows["g"]:rows["g"] + 128, c0:c0 + 512])
                act(S, gzz[:], gzz[:], AF.Sigmoid)
                o_ = og[g % 2]
                tt(S, o_[:], ps_o[:, 0:512], gzz[:], ALU.mult)
                act(S, osq[:], o_[:], AF.Square)
                ps_n = C.bank(7)
                mm(S, ps_n[:, 0:512], K["ones_b"][:], osq[:], R=[osq], W=[ps_n])
                act(S, rstd[:], ps_n[:, 0:512], AF.Sqrt, bias=K["epsb"][:, 0:1], scale=1.0 / 128)
                S.op("dve", lambda e: e.reciprocal(out=rstd[:], in_=rstd[:]), R=[rstd], W=[rstd])
                y_ = yo[g % 2]
                stt(S, y_[:], o_[:], gain[:, 0:1], rstd[:], ALU.mult, ALU.mult)
                S.dma("pool", ydst.ap()[yrow0:yrow0 + 128, c0:c0 + 512], y_[:])


D_MODEL = 4096
D_FF = 11008
FFC = D_FF // NCORES
DC = D_MODEL // NCORES
NLOC = 1411
NSH = 256
PR = dict(q=0, f=128, i=256, g=384, fq=512, fk=640, fv=768, z=896, x=1024, Bm=1152, Cm=1280, ff=1408, dt=1409)


def in_cols(c):
    g = c // 4
    r = lambda a: list(range(a, a + 128))
    loc = (r(0 + 128 * c) + r(1024 + 128 * c) + r(2048 + 128 * c) + r(3072 + 128 * c) +
           r(8032 + 128 * c) + r(9056 + 128 * c) + r(10080 + 128 * c) +
           r(5456 + 128 * c) + r(6480 + 128 * c) + r(7504 + 128 * g) + r(7760 + 128 * g) +
           [11104 + c, 8016 + 2 * c, 8016 + 2 * c + 1])
    sh = [(j if j < 5456 else -1) for j in range(4096 + NSH * c, 4096 + NSH * (c + 1))]
    assert len(loc) == NLOC
    return np.array(loc + sh)


LAYER_INPUTS = [("win", [D_MODEL, NLOC + NSH]), ("wg", [2, D_MODEL, 1024]), ("wb", [D_MODEL, DC]), ("wo", [D_MODEL, DC]),
                ("wfg", [D_MODEL, FFC]), ("wfu", [D_MODEL, FFC]), ("wfd", [D_FF, DC]), ("fcw", [128, 33]),
                ("ang", [128, 4]), ("fng", [128, 4]), ("lbl", [128, 2]), ("hgn", [128, 1]),
                ("qn", [768]), ("kvn", [512]), ("wiq", [768, 1024]), ("wuq", [768, 128]), ("wuk", [512, 128]), ("wuv", [512, 128]),
                ("cw", [128, 12]), ("cb", [128, 3]), ("dtb", [1, 2]), ("alog", [1, 2]), ("dsk", [128, 1]), ("mgain", [128, 8]),
                ("fxb", [1, 1])]


def host_inputs(c, inp, B, SEQ, DEPTH):
    T = B * SEQ
    d = {}
    xT = inp["x"].reshape(T, D_MODEL).T
    d["xT"] = np.ascontiguousarray(xT[DC * c:DC * (c + 1)])
    d["fng"] = np.ascontiguousarray(inp["final_norm"][DC * c:DC * (c + 1)].reshape(4, 128).T)
    d["fnall"] = np.ascontiguousarray(inp["final_norm"])
    tbl, ixm = dsa_host_tables(c, B, SEQ)
    d["tbl"] = tbl
    d["ixmask"] = ixm
    g = c // 4
    cs = slice(DC * c, DC * (c + 1))
    hs = slice(128 * c, 128 * (c + 1))
    for l in range(DEPTH):
        p = "L%d_" % l
        ic = in_cols(c)
        wsl = inp["w_in"][l][:, np.maximum(ic, 0)].copy()
        wsl[:, ic < 0] = 0.0
        d[p + "win"] = np.ascontiguousarray(wsl)
        wg = inp["w_gate"][l]
        d[p + "wg"] = np.ascontiguousarray(np.stack([np.concatenate([wg[0][:, cs], wg[1][:, cs]], 1),
                                                      np.concatenate([wg[2][:, cs], wg[3][:, cs]], 1)], 0))
        wb = inp["w_branch"][l]
        d[p + "wb"] = np.ascontiguousarray(np.concatenate([wb[n][128 * cc:128 * (cc + 1), cs] for cc in range(NCORES) for n in range(4)], 0))
        d[p + "wo"] = np.ascontiguousarray(inp["w_out"][l][:, cs])
        fs = slice(FFC * c, FFC * (c + 1))
        d[p + "wfg"] = np.ascontiguousarray(inp["ffn_w_gate"][l][:, fs])
        d[p + "wfu"] = np.ascontiguousarray(inp["ffn_w_up"][l][:, fs])
        d[p + "wfd"] = np.ascontiguousarray(inp["ffn_w_down"][l][:, cs])
        fc = np.zeros((3, 11 * 128), np.float32)
        fc[:, :FFC] = inp["ffn_conv"][l][:, fs]
        d[p + "fcw"] = np.ascontiguousarray(fc.reshape(3, 11, 128).transpose(2, 1, 0).reshape(128, 33))
        d[p + "ang"] = np.ascontiguousarray(inp["attn_norm"][l][cs].reshape(4, 128).T)
        d[p + "fng"] = np.ascontiguousarray(inp["ffn_norm"][l][cs].reshape(4, 128).T)
        d[p + "lbl"] = np.ascontiguousarray(inp["hgrn_lb_logits"][:, hs].T)
        d[p + "hgn"] = np.ascontiguousarray(inp["hgrn_norm"][l][hs].reshape(128, 1))
        d[p + "qn"] = np.ascontiguousarray(inp["dsa_q_norm"][l])
        d[p + "kvn"] = np.ascontiguousarray(inp["dsa_kv_norm"][l])
        d[p + "wiq"] = np.ascontiguousarray(inp["dsa_w_iq"][l])
        d[p + "wuq"] = np.ascontiguousarray(inp["dsa_w_uq"][l][:, hs])
        d[p + "wuk"] = np.ascontiguousarray(inp["dsa_w_uk"][l][:, hs])
        d[p + "wuv"] = np.ascontiguousarray(inp["dsa_w_uv"][l][:, hs])
        cwf = inp["ssm_conv_w"][l]
        cbf = inp["ssm_conv_b"][l]
        Bs = slice(1024 + 128 * g, 1024 + 128 * (g + 1))
        Cs = slice(1280 + 128 * g, 1280 + 128 * (g + 1))
        cw = np.stack([cwf[:, hs], cwf[:, Bs], cwf[:, Cs]], 0)
        d[p + "cw"] = np.ascontiguousarray(cw.transpose(2, 0, 1).reshape(128, 12))
        d[p + "cb"] = np.ascontiguousarray(np.stack([cbf[hs], cbf[Bs], cbf[Cs]], 1))
        d[p + "dtb"] = np.ascontiguousarray(inp["ssm_dt_bias"][l][2 * c:2 * c + 2].reshape(1, 2))
        d[p + "alog"] = np.ascontiguousarray(inp["ssm_a_log"][l][2 * c:2 * c + 2].reshape(1, 2))
        d[p + "dsk"] = np.ascontiguousarray(np.repeat(inp["ssm_d"][l][2 * c:2 * c + 2], 64).reshape(128, 1))
        d[p + "mgain"] = np.ascontiguousarray(inp["ssm_norm"][l].reshape(8, 128).T)
        d[p + "fxb"] = np.ascontiguousarray(inp["fox_f_bias"][l][c:c + 1].reshape(1, 1))
    return {k: np.asarray(v, dtype=(np.int32 if k == "tbl" else np.float32)) for k, v in d.items()}


def build_model(B, SEQ, DEPTH, topk):
    T = B * SEQ
    nslot = SEQ // 128 // 4
    nc = bass.Bass("TRN2", target_bir_lowering=False)
    ein = lambda n, s, dt=F32: nc.dram_tensor(n, list(s), dt, kind="ExternalInput")
    xT_in = ein("xT", [DC, T])
    fng = ein("fng", [128, 4])
    fnall = ein("fnall", [D_MODEL])
    tbl = ein("tbl", [1, 16], I32)
    ixmask = ein("ixmask", [128, 512])
    class Lazy(dict):
        def __init__(self, l):
            self.l = l
            self.shapes = dict(LAYER_INPUTS)

        def __missing__(self, n):
            self[n] = ein("L%d_%s" % (self.l, n), self.shapes[n])
            return self[n]
    LI = [Lazy(l) for l in range(DEPTH)]
    outT = nc.dram_tensor("outT", [DC, T], F32, kind="ExternalOutput")
    with ExitStack() as es:
        C = Ctx(nc, es)
        S = C.S
        S.kstop = KSTOP
        S.nflush = 0
        dr = lambda name, shape, dt, ag=False: C.dram(name, shape, dt, ag=ag)
        Xloc = dr("Xloc", [DC, T], F32)
        X2loc = dr("X2loc", [DC, T], F32)
        Hloc = dr("Hloc", [DC, T], BF16)
        NSloc, NSall = dr("NSloc", [1, T], F32), dr("NSall", [NCORES, T], F32)
        H = dr("H", [D_MODEL, T], BF16)
        Ploc, SHloc, SH = dr("Ploc", [NLOC, T], F32), dr("SHloc", [NSH, T], F32, ag=True), dr("SH", [NSH * NCORES, T], F32)
        G = dr("G", [4 * DC, T], BF16)
        MBloc, MBall = dr("MBloc", [nslot * 128, SEQ], BF16, ag=True), dr("MBall", [NCORES * nslot * 128, SEQ], BF16)
        Yloc, Y = dr("Yloc", [DC, T], BF16, ag=True), dr("Y", [D_MODEL, T], BF16)
        SSloc, SSall = dr("SSloc", [1, T], F32, ag=True), dr("SSall", [NCORES, T], F32)
        MGloc, MG = dr("MGloc", [DC, T], BF16, ag=True), dr("MG", [D_MODEL, T], BF16)
        Sg = dr("Sg", [FFC, T], BF16)
        ACTloc, ACT = dr("ACTloc", [FFC, T], BF16, ag=True), dr("ACT", [D_FF, T], BF16)

        def body():
            C.es = es
            C._banks = {}
            K = make_consts(C)
            S.flush()
            with C.stage():
                for r0 in range(0, DC, 128):
                    S.dma("sp", Xloc.ap()[r0:r0 + 128, :], xT_in.ap()[r0:r0 + 128, :], R=[], W=[])
                S.flush()

            def norm_to_H(xloc_, gain_in):
                with C.stage():
                    norm_part_stage(C, K, xloc_, NSloc, T)
                    S.allgather(NSloc, NSall)
                    S.flush()
                with C.stage():
                    norm_apply_stage(C, K, xloc_, NSall, gain_in, Hloc, T, BF16)
                    S.allgather(Hloc, H)
                    S.flush()

            for l in range(DEPTH):
                L = LI[l]
                norm_to_H(Xloc, L["ang"])
                with C.stage():
                    W = load_weights(C, L["win"], D_MODEL, NLOC + NSH)
                    ob = [C.sb("a1ob%d" % i, [128, 512], F32) for i in range(3)]
                    cnt = [0]

                    def epi(ps, m, msz, j):
                        b_ = ob[cnt[0] % 3]
                        cnt[0] += 1
                        cp(S, b_[0:msz, :], ps[0:msz, :], eng=("act" if cnt[0] % 2 else "dve"))
                        r0, r1 = m * 128, m * 128 + msz
                        cs_ = slice(j * 512, (j + 1) * 512)
                        if r1 <= NLOC:
                            S.dma("pool", Ploc.ap()[r0:r1, cs_], b_[0:msz, :])
                        elif r0 >= NLOC:
                            S.dma("pool", SHloc.ap()[r0 - NLOC:r1 - NLOC, cs_], b_[0:msz, :])
                        else:
                            S.dma("pool", Ploc.ap()[r0:NLOC, cs_], b_[0:NLOC - r0, :])
                            S.dma("pool", SHloc.ap()[0:r1 - NLOC, cs_], b_[NLOC - r0:msz, :])
                    dense_stage(C, H, D_MODEL, T, W, NLOC + NSH, epi)
                    S.allgather(SHloc, SH)
                    S.flush()
                for half in range(2):
                    with C.stage():
                        W = load_weights(C, L["wg"].ap()[half], D_MODEL, 1024)
                        ob = [C.sb("a2ob%d" % i, [128, 512], BF16) for i in range(3)]
                        cnt = [0]

                        def epi(ps, m, msz, j, half=half):
                            b_ = ob[cnt[0] % 3]
                            cnt[0] += 1
                            act(S, b_[:], ps[:, 0:512], AF.Sigmoid)
                            S.dma("pool", G.ap()[half * 1024 + m * 128:half * 1024 + (m + 1) * 128, j * 512:(j + 1) * 512], b_[:])
                        dense_stage(C, H, D_MODEL, T, W, 1024, epi)
                        S.flush()
                with C.stage():
                    dsa_index_stage(C, K, SH, B, SEQ, topk, L["wiq"], L["qn"], tbl, ixmask, MBloc)
                    S.allgather(MBloc, MBall)
                    S.flush()
                with C.stage():
                    hgrn_stage(C, K, Ploc, PR, B, SEQ, l, L["lbl"], L["hgn"], Yloc, 0)
                    S.flush()
                with C.stage():
                    mamba_stage(C, K, Ploc, PR, B, SEQ, L["cw"], L["cb"], L["dtb"], L["alog"], L["dsk"], Yloc, 256, SSloc)
                    S.flush()
                with C.stage():
                    fox_head(C, K, Ploc, PR["fq"], PR["fk"], PR["fv"], PR["ff"], L["fxb"].ap()[:, :], Yloc, 384, B, SEQ)
                    S.flush()
                with C.stage():
                    dsa_attn_stage(C, K, B, SEQ, SH, MBall, L["wuq"], L["wuk"], L["wuv"], L["qn"], L["kvn"], Yloc, 128)
                    S.allgather(Yloc, Y)
                    S.allgather(SSloc, SSall)
                    S.flush()
                with C.stage():
                    stage_B(C, K, Y, SSall, G, L["wb"], L["mgain"], MGloc, T)
                    S.allgather(MGloc, MG)
                    S.flush()
                with C.stage():
                    W = load_weights(C, L["wo"], D_MODEL, DC)
                    residual_dense(C, MG, D_MODEL, T, W, Xloc, X2loc, 512)
                    S.flush()
                norm_to_H(X2loc, L["fng"])
                with C.stage():
                    stage_D1(C, H, L["wfg"], L["fcw"], Sg, T, SEQ)
                    S.flush()
                with C.stage():
                    W = load_weights(C, L["wfu"], D_MODEL, FFC)
                    sgb = [C.sb("d2sg%d" % i, [128, 512], BF16) for i in range(3)]
                    ob = [C.sb("d2ob%d" % i, [128, 512], BF16) for i in range(3)]
                    cnt = [0]

                    def epi(ps, m, msz, j):
                        i_ = cnt[0] % 3
                        cnt[0] += 1
                        S.dma("sp", sgb[i_][0:msz, :], Sg.ap()[m * 128:m * 128 + msz, j * 512:(j + 1) * 512])
                        tt(S, ob[i_][0:msz, :], ps[0:msz, 0:512], sgb[i_][0:msz, :], ALU.mult)
                        S.dma("pool", ACTloc.ap()[m * 128:m * 128 + msz, j * 512:(j + 1) * 512], ob[i_][0:msz, :])
                    dense_stage(C, H, D_MODEL, T, W, FFC, epi)
                    S.allgather(ACTloc, ACT)
                    S.flush()
                with C.stage():
                    W = load_weights(C, L["wfd"], D_FF, DC)
                    residual_dense(C, ACT, D_FF, T, W, X2loc, Xloc, 256)
                    S.flush()
            with C.stage():
                norm_part_stage(C, K, Xloc, NSloc, T)
                S.allgather(NSloc, NSall)
                S.flush()
            with C.stage():
                norm_apply_stage(C, K, Xloc, NSall, fng, outT, T, F32)
                S.flush()
        try:
            body()
        except StopBuild:
            with C.stage():
                for r0 in range(0, DC, 128):
                    S.dma("sp", outT.ap()[r0:r0 + 128, :], xT_in.ap()[r0:r0 + 128, :], R=[], W=[])
                S.kstop = 0
                S.flush()
    return nc


def residual_dense(C, src, Kdim, T, W, xres, dst, NT):
    S = C.S
    xb = [C.sb("rdx%d" % i, [128, NT], F32) for i in range(3)]
    ob = [C.sb("rdo%d" % i, [128, NT], F32) for i in range(3)]
    cnt = [0]

    def epi(ps, m, msz, j):
        i_ = cnt[0] % 3
        cnt[0] += 1
        S.dma("sp", xb[i_][:], xres.ap()[m * 128:(m + 1) * 128, j * NT:(j + 1) * NT])
        tt(S, ob[i_][:], ps[:, 0:NT], xb[i_][:], ALU.add)
        S.dma("pool", dst.ap()[m * 128:(m + 1) * 128, j * NT:(j + 1) * NT], ob[i_][:])
    dense_stage(C, src, Kdim, T, W, DC, epi, NT=NT)


def stage_B(C, K, Y, SSall, G, wb_d, mgain_d, MGloc, T, NT=512):
    S = C.S
    W = load_weights(C, wb_d, D_MODEL, DC, name="Wb")
    mgain = C.sb("b_mgain", [128, 8], F32)
    S.dma("sp", mgain[:], mgain_d.ap()[:, :])
    yv = Y.ap().rearrange("(kc p) t -> p kc t", p=128)
    abuf = [C.sb("b_abuf%d" % i, [128, 32, NT], BF16) for i in range(2)]
    ssg = [C.sb("b_ss%d" % i, [4, NT], F32) for i in range(2)]
    rstd = [C.sb("b_rstd%d" % i, [128, NT], F32) for i in range(2)]
    gb = [C.sb("b_g%d" % i, [128, NT], BF16) for i in range(4)]
    acc = [C.sb("b_acc%d" % i, [128, NT], F32) for i in range(2)]
    tmpb = [C.sb("b_tmp%d" % i, [128, NT], F32) for i in range(2)]
    ob = [C.sb("b_ob%d" % i, [128, NT], BF16) for i in range(2)]
    gi = 0
    pi = 0
    for j in range(T // NT):
        a = abuf[j % 2]
        cs_ = slice(j * NT, (j + 1) * NT)
        S.dma("sp", a[:, 0:16, :], yv[:, 0:16, cs_], W=[(a.name, 0)])
        S.dma("sp", a[:, 16:32, :], yv[:, 16:32, cs_], W=[(a.name, 1)])
        for g in range(2):
            S.dma("sp", ssg[g][:], SSall.ap()[4 * g:4 * g + 4, cs_])
            ps = C.bank(6 + g)
            mm(S, ps[:, 0:NT], K["ones_f"][0:4, 0:128], ssg[g][:], W=[ps])
            act(S, rstd[g][:], ps[:, 0:NT], AF.Sqrt, bias=K["epsb"][:, 0:1], scale=1.0 / 512)
            S.op("dve", lambda e, g=g: e.reciprocal(out=rstd[g][:], in_=rstd[g][:]), R=[rstd[g]], W=[rstd[g]])
        for cc in range(NCORES):
            q = cc * 4 + 2
            stt(S, a[:, q, :], a[:, q, :], mgain[:, cc:cc + 1], rstd[cc // 4][:], ALU.mult, ALU.mult,
                R=[(a.name, q // 16), mgain, rstd[cc // 4]], W=[(a.name, q // 16)])
        for m in range(DC // 128):
            ac = acc[m % 2]
            for n in range(4):
                gt = gb[gi % 4]
                gi += 1
                S.dma("sp", gt[:], G.ap()[n * DC + m * 128:n * DC + (m + 1) * 128, cs_])
                ps = C.bank(pi % 4)
                pi += 1
                for cc in range(NCORES):
                    q = cc * 4 + n
                    mm(S, ps[:, 0:NT], W[:, q, m * 128:(m + 1) * 128], a[:, q, :], start=(cc == 0), stop=(cc == NCORES - 1),
                       R=[(a.name, q // 16), (W.name, q)], W=[ps])
                if n == 0:
                    tt(S, ac[:], ps[:, 0:NT], gt[:], ALU.mult)
                else:
                    tb_ = tmpb[n % 2]
                    tt(S, tb_[:], ps[:, 0:NT], gt[:], ALU.mult)
                    tt(S, ac[:], ac[:], tb_[:], ALU.add, eng="pool")
            o = ob[m % 2]
            cp(S, o[:], ac[:], eng="act")
            S.dma("pool", MGloc.ap()[m * 128:(m + 1) * 128, cs_], o[:])


def stage_D1(C, H, wfg_d, fcw_d, Sg, T, SEQ, NT=512):
    S = C.S
    W = load_weights(C, wfg_d, D_MODEL, FFC, name="Wfg")
    fcw = C.sb("d1_fcw", [128, 33], F32)
    S.dma("sp", fcw[:], fcw_d.ap()[:, :])
    MT = (FFC + 127) // 128
    carry = C.sb("d1_carry", [128, MT, 2], F32)
    gx = [C.sb("d1_gx%d" % i, [128, 2 + NT], F32) for i in range(3)]
    cv = [C.sb("d1_cv%d" % i, [128, NT], F32) for i in range(2)]
    ob = [C.sb("d1_ob%d" % i, [128, NT], BF16) for i in range(3)]
    cnt = [0]

    def epi(ps, m, msz, j):
        i_ = cnt[0] % 3
        cnt[0] += 1
        g_ = gx[i_]
        if (j * NT) % SEQ == 0:
            memset(S, g_[:, 0:2], 0.0)
        else:
            cp(S, g_[:, 0:2], carry[:, m, :], R=[(carry.name, m)], W=[g_])
        cp(S, g_[0:msz, 2:2 + NT], ps[0:msz, 0:NT], eng="act")
        cp(S, carry[:, m, :], g_[:, NT:NT + 2], R=[g_], W=[(carry.name, m)])
        c_ = cv[i_ % 2]
        ts(S, c_[:], g_[:, 2:2 + NT], fcw[:, 3 * m + 2:3 * m + 3], ALU.mult)
        stt(S, c_[:], g_[:, 1:1 + NT], fcw[:, 3 * m + 1:3 * m + 2], c_[:], ALU.mult, ALU.add)
        stt(S, c_[:], g_[:, 0:NT], fcw[:, 3 * m:3 * m + 1], c_[:], ALU.mult, ALU.add)
        o = ob[i_]
        act(S, o[:], c_[:], AF.Silu)
        S.dma("pool", Sg.ap()[m * 128:m * 128 + msz, j * NT:(j + 1) * NT], o[0:msz, :])
    for i_ in range(3):
        memset(S, gx[i_][:], 0.0)
    dense_stage(C, H, D_MODEL, T, W, FFC, epi, NT=NT)


def final_stage(C, K, X, Xloc, fnall_d, fng_d, outT, T, NT=256):
    S = C.S
    KC = D_MODEL // 128
    xv = X.ap().rearrange("(kc p) t -> p kc t", p=128)
    xl = Xloc.ap().rearrange("(kc p) t -> p kc t", p=128)
    ov = outT.ap().rearrange("(kc p) t -> p kc t", p=128)
    gain = C.sb("f_gain", [128, 4], F32)
    S.dma("sp", gain[:], fng_d.ap()[:, :])
    xb = [C.sb("f_xb%d" % i, [128, KC, NT], F32) for i in range(2)]
    xo = [C.sb("f_xo%d" % i, [128, 4, NT], F32) for i in range(2)]
    sq = C.sb("f_sq", [128, KC, NT], BF16)
    ob = [C.sb("f_ob%d" % i, [128, 4, NT], F32) for i in range(2)]
    rstd = C.sb("f_rstd", [128, NT], F32)
    h2 = KC // 2
    for j in range(T // NT):
        x = xb[j % 2]
        cs_ = slice(j * NT, (j + 1) * NT)
        S.dma("sp", x[:, 0:h2, :], xv[:, 0:h2, cs_], W=[(x.name, 0)])
        S.dma("sp", x[:, h2:KC, :], xv[:, h2:KC, cs_], W=[(x.name, 1)])
        xo_ = xo[j % 2]
        S.dma("sp", xo_[:], xl[:, :, cs_])
        ps = C.bank(j % 2)
        for k in range(KC):
            act(S, sq[:, k, :], x[:, k, :], AF.Square, R=[(x.name, 0 if k < h2 else 1)], W=[(sq.name, k)])
            mm(S, ps[:, 0:NT], K["ones_b"][:], sq[:, k, :], start=(k == 0), stop=(k == KC - 1), R=[(sq.name, k)], W=[ps])
        act(S, rstd[:], ps[:, 0:NT], AF.Sqrt, bias=K["epsb"][:, 0:1], scale=1.0 / D_MODEL)
        S.op("dve", lambda e: e.reciprocal(out=rstd[:], in_=rstd[:]), R=[rstd], W=[rstd])
        o = ob[j % 2]
        for k in range(4):
            stt(S, o[:, k, :], xo_[:, k, :], gain[:, k:k + 1], rstd[:], ALU.mult, ALU.mult, R=[xo_, gain, rstd], W=[(o.name, k)])
        S.dma("pool", ov[:, :, cs_], o[:], R=[(o.name, k) for k in range(4)], W=[])


def norm_part_stage(C, K, xloc, NSloc, T, NT=512):
    S = C.S
    xl = xloc.ap().rearrange("(kc p) t -> p kc t", p=128)
    xb = [C.sb("np_x%d" % i, [128, 4, NT], F32) for i in range(2)]
    sq = [C.sb("np_sq%d" % i, [128, 4, NT], BF16) for i in range(2)]
    sr = [C.sb("np_sr%d" % i, [1, NT], F32) for i in range(2)]
    for j in range(T // NT):
        x = xb[j % 2]
        cs_ = slice(j * NT, (j + 1) * NT)
        S.dma("sp", x[:], xl[:, :, cs_])
        q = sq[j % 2]
        act(S, q[:], x[:], AF.Square)
        ps = C.bank(j % 2)
        for k in range(4):
            mm(S, ps[:, 0:NT], K["ones_b"][:], q[:, k, :], start=(k == 0), stop=(k == 3), R=[q], W=[ps])
        r = sr[j % 2]
        cp(S, r[:], ps[0:1, 0:NT])
        S.dma("pool", NSloc.ap()[0:1, cs_], r[:])


def norm_apply_stage(C, K, xloc, NSall, gain_d, dst, T, out_dt, NT=512):
    S = C.S
    xl = xloc.ap().rearrange("(kc p) t -> p kc t", p=128)
    dv = dst.ap().rearrange("(kc p) t -> p kc t", p=128)
    gain = C.sb("na_gain", [128, 4], F32)
    S.dma("sp", gain[:], gain_d.ap()[:, :])
    xb = [C.sb("na_x%d" % i, [128, 4, NT], F32) for i in range(2)]
    nsb = [C.sb("na_ns%d" % i, [NCORES, NT], F32) for i in range(2)]
    ob = [C.sb("na_o%d" % i, [128, 4, NT], out_dt) for i in range(2)]
    rstd = [C.sb("na_rstd%d" % i, [128, NT], F32) for i in range(2)]
    for j in range(T // NT):
        x = xb[j % 2]
        cs_ = slice(j * NT, (j + 1) * NT)
        S.dma("sp", x[:], xl[:, :, cs_])
        n_ = nsb[j % 2]
        S.dma("sp", n_[:], NSall.ap()[:, cs_])
        ps = C.bank(j % 2)
        mm(S, ps[:, 0:NT], K["ones_f"][0:NCORES, 0:128], n_[:], W=[ps])
        rs = rstd[j % 2]
        act(S, rs[:], ps[:, 0:NT], AF.Sqrt, bias=K["epsb"][:, 0:1], scale=1.0 / D_MODEL)
        S.op("dve", lambda e, rs=rs: e.reciprocal(out=rs[:], in_=rs[:]), R=[rs], W=[rs])
        o = ob[j % 2]
        for k in range(4):
            stt(S, o[:, k, :], x[:, k, :], gain[:, k:k + 1], rs[:], ALU.mult, ALU.mult, R=[x, gain, rs], W=[(o.name, k)])
        S.dma("pool", dv[:, :, cs_], o[:], R=[(o.name, k) for k in range(4)], W=[])


_CACHE = {}


def kernel(**inputs):
    B, SEQ, DEPTH = 2, 4096, 2
    inp = {k: np.asarray(v) for k, v in inputs.items()}
    if "nc" not in _CACHE:
        _CACHE["nc"] = build_model(B, SEQ, DEPTH, min(256, SEQ // 4))
    nc = _CACHE["nc"]
    in_maps = [host_inputs(c, inp, B, SEQ, DEPTH) for c in range(NCORES)]
    res = run_bass_kernel_spmd(nc, in_maps, core_ids=list(range(NCORES)))
    outT = np.concatenate([res.results[c]["outT"] for c in range(NCORES)], 0)
    return np.ascontiguousarray(outT.T).reshape(B, SEQ, D_MODEL).astype(np.float32)
```

```python
import numpy as np
from contextlib import ExitStack
import concourse.bass as bass
import concourse.mybir as mybir
from concourse.bass_utils import run_bass_kernel_spmd

F32 = mybir.dt.float32
BF16 = mybir.dt.bfloat16
AF = mybir.ActivationFunctionType
ALU = mybir.AluOpType
NCORES = 8
EPS = 1e-6
NEG = -30000.0


class Sched:
    ENG = ["pe", "act", "dve", "pool", "sp"]
    NDS = 24

    def __init__(self, nc, es):
        self.nc = nc
        self.es_outer = es
        self.esem = {e: es.enter_context(nc.semaphore("es_" + e)) for e in self.ENG}
        self.ecnt = {e: 0 for e in self.ENG}
        self.dsem = [es.enter_context(nc.semaphore("ds%d" % i)) for i in range(self.NDS)]
        self.dcnt = [0] * self.NDS
        self.dnext = {"sp": 0, "pool": 0}
        self.ccsem = es.enter_context(nc.semaphore("ccs"))
        self.cccnt = 0
        self.waited = {e: {} for e in self.ENG}
        self.reset_stage()

    def reset_stage(self):
        self.prog = {e: [] for e in self.ENG}
        self.lastw = {}
        self.readers = {}

    def semobj(self, sk):
        if sk[0] == "e":
            return self.esem[sk[1]]
        if sk[0] == "d":
            return self.dsem[sk[1]]
        return self.ccsem

    @staticmethod
    def key(x):
        if isinstance(x, (str, tuple)):
            return x
        if hasattr(x, "tensor"):
            return x.tensor.name
        return x.name

    def _deps(self, eng, R, W):
        deps = []
        for k in R:
            t = self.lastw.get(k)
            if t:
                deps.append(t)
        for k in W:
            t = self.lastw.get(k)
            if t:
                deps.append(t)
            deps.extend(self.readers.get(k, {}).values())
        waits = []
        for sk, v in deps:
            if eng == "pe" and sk == ("e", "pe"):
                continue
            if self.waited[eng].get(sk, 0) >= v:
                continue
            self.waited[eng][sk] = v
            waits.append((sk, v))
        return waits

    def _commit(self, tok, R, W):
        for k in R:
            self.readers.setdefault(k, {})[tok[0]] = tok
        for k in W:
            self.lastw[k] = tok
            self.readers[k] = {}

    def op(self, eng, fn, R=(), W=()):
        R = [self.key(x) for x in R]
        W = [self.key(x) for x in W]
        waits = self._deps(eng, R, W)
        self.ecnt[eng] += 1
        tok = (("e", eng), self.ecnt[eng])
        self.prog[eng].append((waits, fn, tok[0], 1))
        self._commit(tok, R, W)
        return tok

    def dma(self, q, out, in_, R=None, W=None, **kw):
        R = [self.key(x) for x in (R if R is not None else [in_])]
        W = [self.key(x) for x in (W if W is not None else [out])]
        R = [k for k in R if not (isinstance(k, str) and k.startswith("D_"))]
        W = [k for k in W if not (isinstance(k, str) and k.startswith("D_"))]
        waits = self._deps(q, R, W)
        half = self.NDS // 2
        slot = (0 if q == "sp" else half) + self.dnext[q]
        self.dnext[q] = (self.dnext[q] + 1) % half
        prev = self.dcnt[slot]
        sk = ("d", slot)
        if prev > 0 and self.waited[q].get(sk, 0) < prev:
            self.waited[q][sk] = prev
            waits.append((sk, prev))
        self.dcnt[slot] += 16
        tok = (sk, self.dcnt[slot])
        self.prog[q].append((waits, lambda e: e.dma_start(out=out, in_=in_, **kw), sk, 16))
        self._commit(tok, R, W)
        return tok

    def wait_tok(self, eng, tok):
        sk, v = tok
        if self.waited[eng].get(sk, 0) >= v:
            return
        self.waited[eng][sk] = v
        self.prog[eng].append(([(sk, v)], None, None, 0))

    def drain_dma(self, engs=("sp", "pool")):
        for e in engs:
            for i in range(self.NDS):
                if self.dcnt[i] > 0:
                    self.wait_tok(e, (("d", i), self.dcnt[i]))

    def allgather(self, src, dst):
        for e in self.ENG:
            if e != "pool" and self.ecnt[e] > 0:
                self.wait_tok("pool", (("e", e), self.ecnt[e]))
        self.drain_dma(("pool",))
        R_, N_ = int(src.shape[0]), int(src.shape[1])
        esz = 4 if src.dtype == F32 else 2
        rb = max(1, min(R_, (512 * 1024) // (N_ * esz)))
        while R_ % rb:
            rb -= 1
        key = (rb, N_, str(src.dtype))
        if not hasattr(self, "agscr"):
            self.agscr = {}
        GR = 8
        if key not in self.agscr:
            i = len(self.agscr)
            self.agscr[key] = [(self.nc.dram_tensor("D_agm%d_%d" % (i, j), [4 * rb, N_], src.dtype),
                                self.nc.dram_tensor("D_ago%d_%d" % (i, j), [8 * rb, N_], src.dtype)) for j in range(GR)]
            self.agtok = getattr(self, "agtok", {})
        g1 = [[0, 1, 2, 3], [4, 5, 6, 7]]
        g2 = [[0, 4], [1, 5], [2, 6], [3, 7]]
        dstv = dst.ap().rearrange("(r k) n -> r k n", k=R_)
        nchunk = R_ // rb

        def coll(a_ap, b_t, grp):
            self.cccnt += 1
            self.prog["pool"].append(([], lambda e: e.collective_compute(
                "AllGather", ALU.bypass, replica_groups=grp, ins=[a_ap.opt()], outs=[b_t.ap().opt()]), ("c",), 1))
            return self.cccnt

        def ccwait(v):
            if self.waited["pool"].get(("c",), 0) < v:
                self.waited["pool"][("c",)] = v
                self.prog["pool"].append(([(("c",), v)], None, None, 0))
        for g0 in range(0, nchunk, GR):
            cis = list(range(g0, min(g0 + GR, nchunk)))
            v1 = {}
            for ci in cis:
                mid, out = self.agscr[key][ci % GR]
                tokk = (key, ci % GR)
                if tokk in self.agtok:
                    self.wait_tok("pool", self.agtok[tokk])
                v1[ci] = coll(src.ap()[ci * rb:(ci + 1) * rb, :], mid, g1)
            v2 = {}
            for ci in cis:
                mid, out = self.agscr[key][ci % GR]
                ccwait(v1[ci])
                v2[ci] = coll(mid.ap(), out, g2)
            for ci in cis:
                mid, out = self.agscr[key][ci % GR]
                ccwait(v2[ci])
                if not NOSCATTER:
                    self.agtok[(key, ci % GR)] = self.dma("pool", dstv[:, ci * rb:(ci + 1) * rb, :],
                                                          out.ap().rearrange("(r k) n -> r k n", k=rb), R=[], W=[])
        self.drain_dma(("pool",))

    def flush(self):
        self.drain_dma(("sp", "pool"))
        nc = self.nc
        prog = self.prog
        with nc.Block() as block:
            def mk(ename):
                def body(q):
                    for waits, fn, sk, inc in prog[ename]:
                        for wsk, v in waits:
                            q.wait_ge(self.semobj(wsk), v)
                        if fn is not None:
                            ins = fn(q)
                            ins.then_inc(self.semobj(sk), inc)
                return body
            block.tensor(mk("pe"))
            block.scalar(mk("act"))
            block.vector(mk("dve"))
            block.gpsimd(mk("pool"))
            block.sync(mk("sp"))
        self.reset_stage()
        self.nflush = getattr(self, "nflush", 0) + 1
        if getattr(self, "kstop", 0) and self.nflush >= self.kstop:
            raise StopBuild()


class StopBuild(Exception):
    pass


KSTOP = 0
NOSCATTER = False


def mm(S, out, lhsT, rhs, start=True, stop=True, R=None, W=None):
    return S.op("pe", lambda e: e.matmul(out, lhsT=lhsT, rhs=rhs, start=start, stop=stop),
                R=R if R is not None else [lhsT, rhs], W=W if W is not None else [out])


def act(S, out, in_, func, bias=None, scale=None, R=None, W=None, eng="act"):
    kw = {}
    rr = [in_]
    if bias is not None:
        kw["bias"] = bias
        if not isinstance(bias, (int, float)):
            rr.append(bias)
    if scale is not None:
        kw["scale"] = scale
        if not isinstance(scale, (int, float)):
            rr.append(scale)
    return S.op("act", lambda e: e.activation(out=out, in_=in_, func=func, **kw),
                R=R if R is not None else rr, W=W if W is not None else [out])


def ts(S, out, in0, s1, op0, s2=None, op1=None, eng="dve", R=None, W=None):
    rr = [in0] + [s for s in (s1, s2) if s is not None and not isinstance(s, (int, float))]
    kw = {}
    if op1 is not None:
        kw["op1"] = op1
    return S.op(eng, lambda e: e.tensor_scalar(out=out, in0=in0, scalar1=s1, scalar2=s2, op0=op0, **kw),
                R=R if R is not None else rr, W=W if W is not None else [out])


def tt(S, out, in0, in1, op, eng="dve", R=None, W=None):
    return S.op(eng, lambda e: e.tensor_tensor(out=out, in0=in0, in1=in1, op=op),
                R=R if R is not None else [in0, in1], W=W if W is not None else [out])


def stt(S, out, in0, scalar, in1, op0, op1, R=None, W=None):
    rr = [in0, in1] + ([scalar] if not isinstance(scalar, (int, float)) else [])
    return S.op("dve", lambda e: e.scalar_tensor_tensor(out=out, in0=in0, scalar=scalar, in1=in1, op0=op0, op1=op1),
                R=R if R is not None else rr, W=W if W is not None else [out])


def cp(S, out, in_, eng="dve", R=None, W=None):
    if eng == "act":
        return S.op("act", lambda e: e.activation(out=out, in_=in_, func=AF.Identity),
                    R=R if R is not None else [in_], W=W if W is not None else [out])
    return S.op(eng, lambda e: e.tensor_copy(out=out, in_=in_),
                R=R if R is not None else [in_], W=W if W is not None else [out])


def memset(S, ap, val, eng="dve"):
    return S.op(eng, lambda e: e.memset(ap, val), R=[], W=[ap])


class Ctx:
    def __init__(self, nc, es):
        self.nc = nc
        self.S = Sched(nc, es)
        self.es = None
        self.uid = 0

    def stage(self):
        self.es = ExitStack()
        self._banks = {}
        ctx = self

        class _Stage:
            def __enter__(self_):
                return ctx.es

            def __exit__(self_, et, ev, tb):
                ctx.es.__exit__(None, None, None)
                return False
        return _Stage()

    def bank(self, i):
        if i not in self._banks:
            self._banks[i] = self.ps("bank%d" % i)
        return self._banks[i]

    def sb(self, name, shape, dt):
        self.uid += 1
        return self.es.enter_context(self.nc.sbuf_tensor("%s_%d" % (name, self.uid), list(shape), dt))

    def ps(self, name, shape=(128, 512), dt=F32):
        self.uid += 1
        return self.es.enter_context(self.nc.psum_tensor("%s_%d" % (name, self.uid), list(shape), dt))

    def dram(self, name, shape, dt, ag=False):
        t = self.nc.dram_tensor("D_" + name, list(shape), dt)
        return t


def make_consts(C):
    S = C.S
    k = {}
    ones_f = C.sb("ones_f", [128, 128], F32)
    memset(S, ones_f[:], 1.0)
    ones_b = C.sb("ones_b", [128, 128], BF16)
    memset(S, ones_b[:], 1.0)
    idf = C.sb("idf", [128, 128], F32)
    S.op("pool", lambda e: e.affine_select(out=idf[:], in_=ones_f[:], pattern=[[-1, 128]], compare_op=ALU.is_equal,
                                           fill=0.0, base=0, channel_multiplier=1), R=[ones_f], W=[idf])
    idb = C.sb("idb", [128, 128], BF16)
    cp(S, idb[:], idf[:])
    zeros_f = C.sb("zeros_f", [128, 128], F32)
    memset(S, zeros_f[:], 0.0)
    cbT_f = C.sb("cbT_f", [128, 128], F32)
    S.op("pool", lambda e: e.affine_select(out=cbT_f[:], in_=zeros_f[:], pattern=[[1, 128]], compare_op=ALU.is_ge,
                                           fill=NEG, base=0, channel_multiplier=-1), R=[zeros_f], W=[cbT_f])
    cbT = C.sb("cbT", [128, 128], BF16)
    cp(S, cbT[:], cbT_f[:])
    cbQ_f = C.sb("cbQ_f", [128, 128], F32)
    S.op("pool", lambda e: e.affine_select(out=cbQ_f[:], in_=zeros_f[:], pattern=[[-1, 128]], compare_op=ALU.is_ge,
                                           fill=NEG, base=0, channel_multiplier=1), R=[zeros_f], W=[cbQ_f])
    m01 = C.sb("m01", [128, 128], F32)
    S.op("pool", lambda e: e.affine_select(out=m01[:], in_=ones_f[:], pattern=[[1, 128]], compare_op=ALU.is_ge,
                                           fill=0.0, base=0, channel_multiplier=-1), R=[ones_f], W=[m01])
    epsb = C.sb("epsb", [128, 1], F32)
    memset(S, epsb[:], EPS)
    k.update(epsb=epsb)
    k.update(ones_f=ones_f, ones_b=ones_b, idf=idf, idb=idb, cbT=cbT, cbT_f=cbT_f, cbQ_f=cbQ_f, zeros_f=zeros_f, m01=m01)
    return k


def load_weights(C, wd, K, Mc, name="W"):
    S = C.S
    KC = K // 128
    W = C.sb(name, [128, KC, Mc], BF16)
    stg = [C.sb(name + "stg%d" % i, [128, Mc], F32) for i in range(2)]
    for k in range(KC):
        st = stg[k % 2]
        S.dma("sp", st[:], wd[k * 128:(k + 1) * 128, :])
        cp(S, W[:, k, :], st[:], eng=("dve" if k % 2 == 0 else "pool"), W=[(W.name, k)])
    return W


def dense_stage(C, src, K, T, W, Mc, epilogue, NT=512, prep=None, wkeys=True):
    S = C.S
    KC = K // 128
    MT = (Mc + 127) // 128
    srcv = src.ap().rearrange("(kc p) t -> p kc t", p=128)
    abuf = [C.sb("abuf%d" % i, [128, KC, NT], BF16) for i in range(2)]
    pss = [C.bank(i) for i in range(4)]
    pi = 0
    for j in range(T // NT):
        a = abuf[j % 2]
        h = KC // 2
        S.dma("sp", a[:, 0:h, :], srcv[:, 0:h, j * NT:(j + 1) * NT], W=[(a.name, 0)])
        S.dma("sp", a[:, h:KC, :], srcv[:, h:KC, j * NT:(j + 1) * NT], W=[(a.name, 1)])
        if prep is not None:
            prep(a, j)
        for m in range(MT):
            msz = min(128, Mc - m * 128)
            ps = pss[pi % 4]
            pi += 1
            for k in range(KC):
                mm(S, ps[0:msz, 0:NT], W[:, k, m * 128:m * 128 + msz], a[:, k, :], start=(k == 0), stop=(k == KC - 1),
                   R=[(a.name, 0 if k < h else 1), (W.name, k)], W=[ps])
            epilogue(ps, m, msz, j)


def norm_stage(C, K, xsrc, gain_d, hdst, D, t0, t1, dst_off, NT=256, out_dt=BF16):
    S = C.S
    KC = D // 128
    xv = xsrc.ap().rearrange("(kc p) t -> p kc t", p=128)
    hv = hdst.ap().rearrange("(kc p) t -> p kc t", p=128)
    gain = C.sb("gain", [128, KC], F32)
    S.dma("sp", gain[:], gain_d.ap().rearrange("(kc p) -> p kc", p=128), allow_slow_non_contiguous=True)
    xb = [C.sb("xb%d" % i, [128, KC, NT], F32) for i in range(2)]
    sq = C.sb("sq", [128, KC, NT], BF16)
    hb = [C.sb("hb%d" % i, [128, KC, NT], out_dt) for i in range(2)]
    rstd = C.sb("rstd", [128, NT], F32)
    pss = [C.bank(i) for i in range(2)]
    nt = (t1 - t0) // NT
    h2 = KC // 2
    for j in range(nt):
        x = xb[j % 2]
        c0 = t0 + j * NT
        S.dma("sp", x[:, 0:h2, :], xv[:, 0:h2, c0:c0 + NT], W=[(x.name, 0)])
        S.dma("sp", x[:, h2:KC, :], xv[:, h2:KC, c0:c0 + NT], W=[(x.name, 1)])
        ps = pss[j % 2]
        for k in range(KC):
            act(S, sq[:, k, :], x[:, k, :], AF.Square, R=[(x.name, 0 if k < h2 else 1)], W=[(sq.name, k)])
            mm(S, ps[:, 0:NT], K["ones_b"][:], sq[:, k, :], start=(k == 0), stop=(k == KC - 1),
               R=[(sq.name, k), K["ones_b"]], W=[ps])
        act(S, rstd[:], ps[:, 0:NT], AF.Sqrt, bias=K["epsb"][:, 0:1], scale=1.0 / D)
        S.op("dve", lambda e: e.reciprocal(out=rstd[:], in_=rstd[:]), R=[rstd], W=[rstd])
        hh = hb[j % 2]
        for k in range(KC):
            stt(S, hh[:, k, :], x[:, k, :], gain[:, k:k + 1], rstd[:], ALU.mult, ALU.mult,
                R=[(x.name, 0 if k < h2 else 1), gain, rstd], W=[(hh.name, k)])
        d0 = dst_off + j * NT
        S.dma("pool", hv[:, 0:h2, d0:d0 + NT], hh[:, 0:h2, :], R=[(hh.name, k) for k in range(0, h2)], W=[])
        S.dma("pool", hv[:, h2:KC, d0:d0 + NT], hh[:, h2:KC, :], R=[(hh.name, k) for k in range(h2, KC)], W=[])


def attn_core(C, K, qT, kT, vtok, nblk, bias_fn, out_cb, act_bias=None, pfx="at", pre_tb=None):
    S = C.S
    sps = [C.bank(0), C.bank(1)]
    ops = [C.bank(2), C.bank(3)]
    dps = [C.bank(4), C.bank(5)]
    pT = [C.sb(pfx + "pT%d" % i, [128, 512], BF16) for i in range(3)]
    rden = [C.sb(pfx + "rden%d" % i, [128, 128], F32) for i in range(2)]
    gi = 0
    for tb in range(nblk):
        o_ps = ops[tb % 2]
        d_ps = dps[tb % 2]
        if pre_tb is not None:
            pre_tb(tb)
        for g0 in range(0, tb + 1, 4):
            gs = list(range(g0, min(g0 + 4, tb + 1)))
            sp_ = sps[gi % 2]
            pt = pT[gi % 3]
            gi += 1
            for i, sb in enumerate(gs):
                extra = bias_fn(tb, sb)
                out = sp_[:, i * 128:(i + 1) * 128]
                mm(S, out, kT[:, sb * 128:(sb + 1) * 128], qT[:, tb * 128:(tb + 1) * 128],
                   start=True, stop=(len(extra) == 0), W=[sp_])
                for ei, (l, r, rk) in enumerate(extra):
                    mm(S, out, l, r, start=False, stop=(ei == len(extra) - 1), R=rk, W=[sp_])
            n = len(gs) * 128
            bkw = {} if act_bias is None else {"bias": act_bias(tb)}
            act(S, pt[:, 0:n], sp_[:, 0:n], AF.Exp, **bkw)
            for i, sb in enumerate(gs):
                mm(S, o_ps[:, 0:128], vtok[:, sb, :], pt[:, i * 128:(i + 1) * 128], start=(sb == 0), stop=(sb == tb),
                   R=[vtok, pt], W=[o_ps])
                mm(S, d_ps[:, 0:128], K["ones_b"][:], pt[:, i * 128:(i + 1) * 128], start=(sb == 0), stop=(sb == tb),
                   R=[pt], W=[d_ps])
        rd = rden[tb % 2]
        S.op("dve", lambda e, rd=rd, d_ps=d_ps: e.reciprocal(out=rd[:], in_=d_ps[:, 0:128]), R=[d_ps], W=[rd])
        out_cb(tb, o_ps, rd)


def to_tokmajor(C, K, srcT, nblk, name):
    S = C.S
    dst = C.sb(name, [128, nblk, 128], BF16)
    pss = [C.bank(0), C.bank(1)]
    for g in range(0, nblk, 4):
        ps = pss[(g // 4) % 2]
        n = min(4, nblk - g)
        for i in range(n):
            mm(S, ps[:, i * 128:(i + 1) * 128], srcT[:, (g + i) * 128:(g + i + 1) * 128], K["idb"][:], W=[ps])
        cp(S, dst[:, g:g + n, :], ps[:, 0:n * 128].rearrange("p (a b) -> p a b", b=128), eng="act", W=[(dst.name, g)])
    return dst


def fox_head(C, K, PJ, rq, rk, rv, rf, fbias_d, ydst, yrow0, B, SEQ):
    S = C.S
    nblk = SEQ // 128
    PJa = PJ.ap()
    stg = C.sb("fx_stg", [128, SEQ], F32)
    qT = C.sb("fx_qT", [128, SEQ], BF16)
    kT = C.sb("fx_kT", [128, SEQ], BF16)
    vT = C.sb("fx_vT", [128, SEQ], BF16)
    frow = C.sb("fx_frow", [1, SEQ], F32)
    erow = C.sb("fx_erow", [1, SEQ], F32)
    lnrow = C.sb("fx_lnrow", [1, SEQ], F32)
    ncum = C.sb("fx_ncum", [1, SEQ], F32)
    onesrow = C.sb("fx_ones", [1, SEQ], F32)
    fb = C.sb("fx_fb", [1, 1], F32)
    nfb = C.sb("fx_nfb", [1, 1], F32)
    cmidbc = C.sb("fx_cmid", [128, nblk], F32)
    cps = C.bank(6)
    ob = [C.sb("fx_ob%d" % i, [128, 128], BF16) for i in range(2)]
    memset(S, onesrow[:], 1.0)
    S.dma("sp", fb[:], fbias_d)
    ts(S, nfb[:], fb[:], -1.0, ALU.mult)
    for b in range(B):
        t0 = b * SEQ
        S.dma("sp", stg[:], PJa[rq:rq + 128, t0:t0 + SEQ])
        ts(S, qT[:], stg[:], 128 ** -0.5, ALU.mult)
        S.dma("sp", stg[:], PJa[rk:rk + 128, t0:t0 + SEQ])
        cp(S, kT[:], stg[:], eng="pool")
        S.dma("sp", stg[:], PJa[rv:rv + 128, t0:t0 + SEQ])
        cp(S, vT[:], stg[:], eng="dve")
        vtok = to_tokmajor(C, K, vT, nblk, "fx_vtok%d" % b)
        S.dma("sp", frow[:], PJa[rf:rf + 1, t0:t0 + SEQ])
        act(S, erow[:], frow[:], AF.Exp, bias=nfb[:, 0:1], scale=-1.0)
        act(S, lnrow[:], erow[:], AF.Ln, bias=1.0)
        S.op("dve", lambda e: e.tensor_tensor_scan(out=ncum[:], data0=onesrow[:], data1=lnrow[:], initial=0.0,
                                                   op0=ALU.mult, op1=ALU.add), R=[onesrow, lnrow], W=[ncum])
        mids = ncum[0:1, :].rearrange("p (a b) -> p a b", b=128)[:, :, 64:65].rearrange("p a b -> p (a b)")
        mm(S, cps[:, 0:nblk], K["ones_f"][0:1, :], mids, R=[ncum], W=[cps])
        ts(S, cmidbc[:], cps[:, 0:nblk], -1.0, ALU.mult)

        def bias_fn(tb, sb):
            ex = [(ncum[0:1, sb * 128:(sb + 1) * 128], K["ones_f"][0:1, :], [ncum])]
            if sb == tb:
                ex.append((K["idb"][:], K["cbT"][:], []))
            return ex

        def out_cb(tb, o_ps, rd, t0=t0):
            o = ob[tb % 2]
            tt(S, o[:], o_ps[:, 0:128], rd[:], ALU.mult)
            S.dma("pool", ydst.ap()[yrow0:yrow0 + 128, t0 + tb * 128:t0 + (tb + 1) * 128], o[:])

        attn_core(C, K, qT, kT, vtok, nblk, bias_fn, out_cb, act_bias=lambda tb: cmidbc[:, tb:tb + 1], pfx="fx%d" % b)


I32 = mybir.dt.int32


def dyn_dma(S, out, tensor, tbl_ap, pattern, R, W):
    if not hasattr(S, "dynregs"):
        S.dynregs = [S.es_outer.enter_context(S.nc.gpsimd.register("dynr%d" % i)) for i in range(8)]
        S.dyni = 0
    gr = S.dynregs[S.dyni % 8]
    S.dyni += 1

    def fn(e):
        e.reg_load(gr, tbl_ap)
        return e.dma_start(out=out, in_=bass.AP(tensor, gr, pattern))
    Rk = [S.key(x) for x in R]
    Wk = [S.key(x) for x in W]
    waits = S._deps("pool", Rk, Wk)
    half = S.NDS // 2
    slot = half + S.dnext["pool"]
    S.dnext["pool"] = (S.dnext["pool"] + 1) % half
    prev = S.dcnt[slot]
    sk = ("d", slot)
    if prev > 0 and S.waited["pool"].get(sk, 0) < prev:
        S.waited["pool"][sk] = prev
        waits.append((sk, prev))
    S.dcnt[slot] += 16
    tok = (sk, S.dcnt[slot])
    S.prog["pool"].append((waits, fn, sk, 16))
    S._commit(tok, Rk, Wk)


def norm_tile(C, K, x, KC, n, gain, o, bank, nfeat, scratch, okey=None):
    S = C.S
    sq, rstd = scratch
    for k in range(KC):
        act(S, sq[:, k, :], x[:, k, :], AF.Square, R=[x], W=[(sq.name, k)])
        mm(S, bank[:, 0:n], K["ones_b"][:], sq[:, k, :], start=(k == 0), stop=(k == KC - 1), R=[(sq.name, k)], W=[bank])
    act(S, rstd[:], bank[:, 0:n], AF.Sqrt, bias=K["epsb"][:, 0:1], scale=1.0 / nfeat)
    S.op("dve", lambda e: e.reciprocal(out=rstd[:], in_=rstd[:]), R=[rstd], W=[rstd])
    for k in range(KC):
        stt(S, o[:, k, :], x[:, k, :], gain[:, k:k + 1], rstd[:], ALU.mult, ALU.mult, R=[x, gain, rstd],
            W=[okey if okey is not None else o])


def dsa_index_stage(C, K, SH, B, SEQ, topk, w_iq_d, qn_d, tbl_d, ixmask_d, MBloc, dbg=None):
    S = C.S
    T = B * SEQ
    nblk = SEQ // 128
    nslot = nblk // 4
    Wiq = load_weights(C, w_iq_d, 768, 1024, name="Wiq")
    tbl = C.sb("ix_tbl", [1, 16], I32)
    S.dma("sp", tbl[:], tbl_d.ap()[:, :])
    ixm = C.sb("ix_mask", [128, 512], F32)
    S.dma("sp", ixm[:], ixmask_d.ap()[:, :])
    ixmb = C.sb("ix_maskb", [128, 512], BF16)
    cp(S, ixmb[:], ixm[:])
    gain = C.sb("ix_gain", [128, 6], F32)
    S.dma("sp", gain[:], qn_d.ap().rearrange("(kc p) -> p kc", p=128), allow_slow_non_contiguous=True)
    kstg = C.sb("kstg", [128, SEQ], F32)
    kidx2 = C.sb("kidx2", [128, SEQ], BF16)
    dyn_dma(S, kstg[0:64, :], SH, tbl[0:1, 0:1], [[T, 64], [1, SEQ]], R=[tbl], W=[kstg])
    S.dma("sp", kstg[64:128, :], kstg[0:64, :])
    cp(S, kidx2[:], kstg[:])
    acc = C.sb("ix_acc", [128, SEQ], F32)
    work = [C.sb("ix_work%d" % i, [128, SEQ], F32) for i in range(2)]
    tmp = [C.sb("ix_tmp%d" % i, [128, 512], F32) for i in range(3)]
    cqx_all = C.sb("ix_cqx", [128, 6, nslot, 128], F32)
    widx_all = C.sb("ix_widxT", [16, nslot, 128], F32)
    for kc in range(6):
        dyn_dma(S, cqx_all[:, kc, :, :], SH, tbl[0:1, 1 + kc:2 + kc], [[T, 128], [512, nslot], [1, 128]], R=[tbl], W=[cqx_all])
    dyn_dma(S, widx_all[:], SH, tbl[0:1, 7:8], [[T, 16], [512, nslot], [1, 128]], R=[tbl], W=[widx_all])
    cqn = C.sb("ix_cqn", [128, 6, 128], BF16)
    qiT = C.sb("qiT", [128, 8, 128], BF16)
    wt = C.sb("ix_wt", [128, 16], F32)
    wabs = C.sb("ix_wabs", [128, 16], F32)
    wsgn = C.sb("ix_wsgn", [128, 16], F32)
    m8 = C.sb("ix_m8", [128, 8], F32)
    MBt = [C.sb("ix_MBt%d" % i, [128, SEQ], BF16) for i in range(2)]
    ti = 0
    nsc = (C.sb("ix_nsq", [128, 6, 128], BF16), C.sb("ix_nrstd", [128, 128], F32))
    for i in range(nslot):
        n = (4 * i + 4) * 128
        cx = cqx_all[:, :, i, :]
        wx = widx_all[:, i, :]
        norm_tile(C, K, cx, 6, 128, gain, cqn, C.bank(7), 768, nsc)
        for half in range(2):
            ps = C.bank(half)
            for pp in range(4):
                p = half * 4 + pp
                for k in range(6):
                    mm(S, ps[:, pp * 128:(pp + 1) * 128], Wiq[:, k, p * 128:(p + 1) * 128], cqn[:, k, :],
                       start=(k == 0), stop=(k == 5), R=[cqn, (Wiq.name, k)], W=[ps])
            cp(S, qiT[:, half * 4:(half + 1) * 4, :], ps[:, 0:512].rearrange("p (a b) -> p a b", b=128), eng="act", W=[qiT])
        ps = C.bank(2)
        mm(S, ps[:, 0:16], wx, K["idf"][0:16, 0:16], W=[ps])
        ts(S, wt[:], ps[:, 0:16], 1.0 / 32.0, ALU.mult)
        act(S, wabs[:], wt[:], AF.Abs)
        act(S, wsgn[:], wt[:], AF.Sign)
        nkc = n // 512
        for kc in range(nkc):
            for h in range(16):
                p, half = h // 2, h % 2
                ps = C.bank(3 + (ti % 4))
                tm = tmp[ti % 3]
                ti += 1
                mm(S, ps[:, 0:512], qiT[half * 64:(half + 1) * 64, p, :], kidx2[half * 64:(half + 1) * 64, kc * 512:(kc + 1) * 512],
                   R=[qiT, kidx2], W=[ps])
                act(S, tm[:], ps[:, 0:512], AF.Relu, scale=wabs[:, h:h + 1])
                a = acc[:, kc * 512:(kc + 1) * 512]
                if h == 0:
                    ts(S, a, tm[:], wsgn[:, h:h + 1], ALU.mult, W=[(acc.name, kc)])
                else:
                    stt(S, a, tm[:], wsgn[:, h:h + 1], a, ALU.mult, ALU.add, R=[tm, wsgn, (acc.name, kc)], W=[(acc.name, kc)])
        allk = [(acc.name, kc) for kc in range(nkc)]
        tail = acc[:, n - 512:n]
        tt(S, tail, tail, ixm[:], ALU.add, R=allk + [ixm], W=allk)
        mb = MBt[i % 2]
        rounds = topk // 8
        src = acc
        spc = C.sb("ix_spc", [128, 256], F32)

        def spacer():
            S.op("dve", lambda e: e.tensor_copy(out=spc[:], in_=ixm[:, 0:256]), R=[], W=[])
        spacer()
        for r in range(rounds):
            S.op("dve", lambda e, src=src, n=n: e.max(out=m8[:], in_=src[:, 0:n]), R=(allk if src is acc else [src]), W=[m8])
            spacer()
            if r < rounds - 1:
                dst = work[r % 2]
                S.op("dve", lambda e, src=src, dst=dst, n=n: e.match_replace(out=dst[:, 0:n], in_to_replace=m8[:], in_values=src[:, 0:n],
                                                                       imm_value=-1.0e30),
                     R=(allk if src is acc else [src]) + [m8], W=[dst])
                spacer()
                src = dst
        ts(S, work[0][:, 0:n], acc[:, 0:n], m8[:, 7:8], ALU.is_lt, R=allk + [m8], W=[work[0]])
        ts(S, mb[:, 0:n], work[0][:, 0:n], NEG, ALU.mult, R=[work[0]], W=[mb])
        tt(S, mb[:, n - 512:n], mb[:, n - 512:n], ixmb[:], ALU.min)
        S.dma("sp", MBloc.ap()[i * 128:(i + 1) * 128, 0:n], mb[:, 0:n])
        if dbg is not None:
            S.dma("pool", dbg.ap()[i * 128:(i + 1) * 128, 0:n], mb[:, 0:n])
            S.dma("sp", dbg.ap()[(nslot + i) * 128:(nslot + i + 1) * 128, 0:n], acc[:, 0:n], R=allk, W=[])


def dsa_attn_stage(C, K, B, SEQ, SH, MBall, wuq_d, wuk_d, wuv_d, qn_d, kvn_d, ydst, yrow0):
    S = C.S
    nblk = SEQ // 128
    nslot = nblk // 4
    NT = 512
    Wuq = load_weights(C, wuq_d, 768, 128, name="Wuq")
    Wuk = load_weights(C, wuk_d, 512, 128, name="Wuk")
    Wuv = load_weights(C, wuv_d, 512, 128, name="Wuv")
    gq = C.sb("ds_gq", [128, 6], F32)
    S.dma("sp", gq[:], qn_d.ap().rearrange("(kc p) -> p kc", p=128), allow_slow_non_contiguous=True)
    gkv = C.sb("ds_gkv", [128, 4], F32)
    S.dma("sp", gkv[:], kvn_d.ap().rearrange("(kc p) -> p kc", p=128), allow_slow_non_contiguous=True)
    xq = [C.sb("ds_xq%d" % i, [128, 6, NT], F32) for i in range(2)]
    xkv = [C.sb("ds_xkv%d" % i, [128, 4, NT], F32) for i in range(2)]
    lq = C.sb("ds_lq", [128, 6, NT], BF16)
    lkv = C.sb("ds_lkv", [128, 4, NT], BF16)
    qT = C.sb("ds_qT", [128, SEQ], BF16)
    kT = C.sb("ds_kT", [128, SEQ], BF16)
    vtok = C.sb("ds_vtok", [128, nblk, 128], BF16)
    MBt = [C.sb("ds_MBt%d" % i, [128, SEQ], BF16) for i in range(2)]
    ob = [C.sb("ds_ob%d" % i, [128, 128], BF16) for i in range(2)]
    shv = SH.ap()
    nsc1 = (C.sb("ds_nsq1", [128, 6, NT], BF16), C.sb("ds_nrstd1", [128, NT], F32))
    nsc2 = (C.sb("ds_nsq2", [128, 4, NT], BF16), C.sb("ds_nrstd2", [128, NT], F32))
    for b in range(B):
        t0 = b * SEQ
        for j in range(SEQ // NT):
            c0 = t0 + j * NT
            x1 = xq[j % 2]
            x2 = xkv[j % 2]
            S.dma("sp", x1[:], shv[0:768, c0:c0 + NT].rearrange("(kc p) t -> p kc t", p=128))
            S.dma("sp", x2[:], shv[768:1280, c0:c0 + NT].rearrange("(kc p) t -> p kc t", p=128))
            norm_tile(C, K, x1, 6, NT, gq, lq, C.bank(6), 768, nsc1)
            norm_tile(C, K, x2, 4, NT, gkv, lkv, C.bank(7), 512, nsc2)
            ps = C.bank(j % 2)
            for k in range(6):
                mm(S, ps[:, 0:NT], Wuq[:, k, :], lq[:, k, :], start=(k == 0), stop=(k == 5), R=[lq, (Wuq.name, k)], W=[ps])
            ts(S, qT[:, j * NT:(j + 1) * NT], ps[:, 0:NT], 128 ** -0.5, ALU.mult)
            ps = C.bank(2 + j % 2)
            for k in range(4):
                mm(S, ps[:, 0:NT], Wuk[:, k, :], lkv[:, k, :], start=(k == 0), stop=(k == 3), R=[lkv, (Wuk.name, k)], W=[ps])
            cp(S, kT[:, j * NT:(j + 1) * NT], ps[:, 0:NT], eng="act")
            ps = C.bank(4 + j % 2)
            for i in range(4):
                for k in range(4):
                    mm(S, ps[:, i * 128:(i + 1) * 128], lkv[:, k, i * 128:(i + 1) * 128], Wuv[:, k, :],
                       start=(k == 0), stop=(k == 3), R=[lkv, (Wuv.name, k)], W=[ps])
            cp(S, vtok[:, j * 4:(j + 1) * 4, :], ps[:, 0:512].rearrange("p (a b) -> p a b", b=128), eng="act", W=[vtok])

        def pre_tb(tb, b=b):
            r = (2 * (tb % 4) + b) * nslot + tb // 4
            S.dma("sp", MBt[tb % 2][:, 0:(tb + 1) * 128], MBall.ap()[r * 128:(r + 1) * 128, 0:(tb + 1) * 128])

        def bias_fn(tb, sb):
            return [(MBt[tb % 2][:, sb * 128:(sb + 1) * 128], K["idb"][:], [MBt[tb % 2]])]

        def out_cb(tb, o_ps, rd, t0=t0):
            o = ob[tb % 2]
            tt(S, o[:], o_ps[:, 0:128], rd[:], ALU.mult)
            S.dma("pool", ydst.ap()[yrow0:yrow0 + 128, t0 + tb * 128:t0 + (tb + 1) * 128], o[:])

        attn_core(C, K, qT, kT, vtok, nblk, bias_fn, out_cb, pfx="ds%d" % b, pre_tb=pre_tb)


def dsa_host_tables(c, B, SEQ):
    T = B * SEQ
    nslot = (SEQ // 128) // 4
    b, d = c % 2, c // 2
    tbl = np.zeros((1, 16), np.int32)
    tbl[0, 0] = 1280 * T + b * SEQ
    for kc in range(6):
        tbl[0, 1 + kc] = kc * 128 * T + b * SEQ + d * 128
    tbl[0, 7] = 1344 * T + b * SEQ + d * 128
    r = np.arange(128)[:, None]
    j = np.arange(512)[None, :]
    ixmask = np.where(j <= d * 128 + r, 0.0, NEG).astype(np.float32)
    return tbl, ixmask


def mamba_stage(C, K, P, rows, B, SEQ, cw_d, cb_d, dtb_d, alog_d, dsk_d, ydst, yrow0, SSloc):
    S = C.S
    Pa = P.ap()
    nch = SEQ // 128
    cw = C.sb("mb_cw", [128, 12], F32)
    S.dma("sp", cw[:], cw_d.ap()[:, :])
    cb = C.sb("mb_cb", [128, 3], F32)
    S.dma("sp", cb[:], cb_d.ap()[:, :])
    dsk = C.sb("mb_dsk", [128, 1], F32)
    S.dma("sp", dsk[:], dsk_d.ap()[:, :])
    dtb = C.sb("mb_dtb", [33, 1], F32)
    nA = C.sb("mb_nA", [33, 1], F32)
    memset(S, dtb[:], 0.0)
    memset(S, nA[:], 0.0)
    for hh in range(2):
        S.dma("sp", dtb[32 * hh:32 * hh + 1, :], dtb_d.ap()[0:1, hh:hh + 1])
        S.dma("sp", nA[32 * hh:32 * hh + 1, :], alog_d.ap()[0:1, hh:hh + 1])
    act(S, nA[:], nA[:], AF.Exp)
    ts(S, nA[:], nA[:], -1.0, ALU.mult)
    raw = C.sb("mb_raw", [128, 3 + SEQ], F32)
    cacc = C.sb("mb_cacc", [128, SEQ], F32)
    xsb = C.sb("mb_xsb", [128, SEQ], BF16)
    Bc = C.sb("mb_Bc", [128, SEQ], BF16)
    Cc = C.sb("mb_Cc", [128, SEQ], BF16)
    Ct = [C.sb("mb_Ct%d" % h, [128, SEQ], BF16) for h in range(2)]
    rA = C.sb("mb_rA", [33, SEQ], F32)
    rB = C.sb("mb_rB", [33, SEQ], F32)
    rC = C.sb("mb_rC", [33, SEQ], F32)
    rD = C.sb("mb_rD", [33, SEQ], F32)
    rE = C.sb("mb_rE", [33, SEQ], BF16)
    ecl = C.sb("mb_ecl", [128, 2, nch], F32)
    wtok = C.sb("mb_wtok", [128, 2 * nch], F32)
    stF = C.sb("mb_stF", [128, 128], F32)
    stT = C.sb("mb_stT", [128, 128], BF16)
    LT = [C.sb("mb_LT%d" % i, [128, 256], F32) for i in range(2)]
    MT = [C.sb("mb_MT%d" % i, [128, 256], BF16) for i in range(2)]
    Bw = [C.sb("mb_Bw%d" % i, [128, 256], BF16) for i in range(2)]
    zr = [C.sb("mb_zr%d" % i, [128, 512], F32) for i in range(2)]
    zs = [C.sb("mb_zs%d" % i, [128, 512], F32) for i in range(2)]
    y1 = [C.sb("mb_y1%d" % i, [128, 128], F32) for i in range(2)]
    ysq = [C.sb("mb_ysq%d" % i, [128, 128], BF16) for i in range(2)]
    yo = [C.sb("mb_yo%d" % i, [128, 128], BF16) for i in range(2)]
    ssrow = [C.sb("mb_ssrow%d" % i, [1, 512], F32) for i in range(2)]
    memset(S, raw[:, 0:3], 0.0)
    memset(S, rE[:], 1.0)
    memset(S, rE[:, :].rearrange("p (c l) -> p c l", l=128)[:, :, 0:1], 0.0)
    for b in range(B):
        t0 = b * SEQ
        memset(S, raw[:, 0:3], 0.0)
        for g, (r0, dst) in enumerate(((rows["x"], xsb), (rows["Bm"], Bc), (rows["Cm"], Cc))):
            S.dma("sp", raw[:, 3:3 + SEQ], Pa[r0:r0 + 128, t0:t0 + SEQ])
            ts(S, cacc[:], raw[:, 0:SEQ], cw[:, 4 * g:4 * g + 1], ALU.mult, cb[:, g:g + 1], ALU.add)
            for k in range(1, 4):
                stt(S, cacc[:], raw[:, k:k + SEQ], cw[:, 4 * g + k:4 * g + k + 1], cacc[:], ALU.mult, ALU.add)
            act(S, dst[:], cacc[:], AF.Silu)
        memset(S, rA[:], 0.0)
        for hh in range(2):
            S.dma("sp", rA[32 * hh:32 * hh + 1, :], Pa[rows["dt"] + hh:rows["dt"] + hh + 1, t0:t0 + SEQ])
        act(S, rA[:], rA[:], AF.Exp, bias=dtb[:, 0:1])
        act(S, rA[:], rA[:], AF.Ln, bias=1.0)
        act(S, rB[:], rA[:], AF.Ln)
        ts(S, rC[:], rA[:], nA[:, 0:1], ALU.mult)
        S.op("dve", lambda e: e.tensor_tensor_scan(out=rD[:], data0=rE[:], data1=rC[:], initial=0.0,
                                                   op0=ALU.mult, op1=ALU.add), R=[rE, rC], W=[rD])
        tt(S, rB[:], rB[:], rD[:], ALU.subtract)
        act(S, rA[:], rD[:], AF.Exp)
        last = rD[:, :].rearrange("p (c l) -> p c l", l=128)[:, :, 127:128].broadcast_to([33, nch, 128])
        tt(S, rC[:, :].rearrange("p (c l) -> p c l", l=128), rB[:, :].rearrange("p (c l) -> p c l", l=128), last, ALU.add,
           R=[rB, rD], W=[rC])
        act(S, rC[:], rC[:], AF.Exp)
        Ebc = raw
        for hh in range(2):
            p0 = 32 * hh
            for j in range(SEQ // 512):
                ps = C.bank(j % 2)
                mm(S, ps[:, 0:512], K["ones_f"][p0:p0 + 1, 0:128], rA[p0:p0 + 1, j * 512:(j + 1) * 512], R=[rA], W=[ps])
                cp(S, Ebc[:, j * 512:(j + 1) * 512], ps[:, 0:512], eng="act", W=[Ebc])
            tt(S, Ct[hh][:], Cc[:], Ebc[:, 0:SEQ], ALU.mult)
            cp(S, ecl[:, hh, :], Ebc[:, 0:SEQ].rearrange("p (c l) -> p c l", l=128)[:, :, 127:128].rearrange("p c l -> p (c l)"),
               R=[Ebc], W=[ecl])
            ps = C.bank(0)
            for j in range(nch):
                mm(S, ps[:, hh * nch + j:hh * nch + j + 1], rC[p0:p0 + 1, j * 128:(j + 1) * 128], K["ones_f"][p0:p0 + 1, 0:1],
                   R=[rC], W=[ps])
            cp(S, wtok[:, hh * nch:(hh + 1) * nch], ps[:, hh * nch:(hh + 1) * nch])
        xtok = to_tokmajor(C, K, xsb, nch, "mb_xtok%d" % b)
        Btok = to_tokmajor(C, K, Bc, nch, "mb_Btok%d" % b)
        memset(S, stF[:], 0.0)
        memset(S, stT[:], 0.0)
        ps_ss = C.bank(7)
        for j in range(nch):
            ck = slice(j * 128, (j + 1) * 128)
            if j % 4 == 0:
                zz = zr[(j // 4) % 2]
                S.dma("sp", zz[:], Pa[rows["z"]:rows["z"] + 128, t0 + j * 128:t0 + j * 128 + 512])
                zq = zs[(j // 4) % 2]
                act(S, zq[:], zz[:], AF.Silu)
            ps_g = C.bank(2)
            ps_d = C.bank(3)
            for hh in range(2):
                p0 = 32 * hh
                mm(S, ps_g[:, hh * 128:(hh + 1) * 128], Bc[:, ck], Cc[:, ck], W=[ps_g])
                o_ = ps_d[:, hh * 128:(hh + 1) * 128]
                mm(S, o_, K["ones_f"][p0:p0 + 1, 0:128], rD[p0:p0 + 1, ck], start=True, stop=False, R=[rD], W=[ps_d])
                mm(S, o_, rB[p0:p0 + 1, ck], K["ones_f"][p0:p0 + 1, 0:128], start=False, stop=False, R=[rB], W=[ps_d])
                mm(S, o_, K["idb"][:], K["cbT"][:], start=False, stop=True, R=[], W=[ps_d])
            lt = LT[j % 2]
            act(S, lt[:], ps_d[:, 0:256], AF.Exp)
            mt = MT[j % 2]
            tt(S, mt[:], ps_g[:, 0:256], lt[:], ALU.mult)
            ps_y = C.bank(4 + j % 2)
            for hh in range(2):
                hs = slice(hh * 64, (hh + 1) * 64)
                mm(S, ps_y[hs, 0:128], xtok[:, j, hs], mt[:, hh * 128:(hh + 1) * 128], start=True, stop=(j == 0),
                   R=[xtok, mt], W=[ps_y])
                if j > 0:
                    mm(S, ps_y[hs, 0:128], stT[:, hs], Ct[hh][:, ck], start=False, stop=True, R=[stT, Ct[hh]], W=[ps_y])
            bw = Bw[j % 2]
            ps_s = C.bank(6)
            for hh in range(2):
                hs = slice(hh * 64, (hh + 1) * 64)
                ts(S, bw[:, hh * 128:(hh + 1) * 128], Btok[:, j, :], wtok[:, hh * nch + j:hh * nch + j + 1], ALU.mult,
                   R=[Btok, wtok], W=[bw])
                mm(S, ps_s[:, hs], bw[:, hh * 128:(hh + 1) * 128], xtok[:, j, hs], R=[bw, xtok], W=[ps_s])
            for hh in range(2):
                hs = slice(hh * 64, (hh + 1) * 64)
                stt(S, stF[:, hs], stF[:, hs], ecl[:, hh, j:j + 1], ps_s[:, hs], ALU.mult, ALU.add, R=[stF, ecl, ps_s], W=[stF])
            cp(S, stT[:], stF[:])
            a1 = y1[j % 2]
            stt(S, a1[:], xsb[:, ck], dsk[:, 0:1], ps_y[:, 0:128], ALU.mult, ALU.add, R=[xsb, dsk, ps_y], W=[a1])
            tt(S, a1[:], a1[:], zs[(j // 4) % 2][:, (j % 4) * 128:(j % 4 + 1) * 128], ALU.mult)
            sq_ = ysq[j % 2]
            act(S, sq_[:], a1[:], AF.Square)
            mm(S, ps_ss[:, (j % 4) * 128:(j % 4 + 1) * 128], K["ones_b"][:], sq_[:], R=[sq_], W=[ps_ss])
            o = yo[j % 2]
            cp(S, o[:], a1[:], eng="pool")
            S.dma("pool", ydst.ap()[yrow0:yrow0 + 128, t0 + j * 128:t0 + (j + 1) * 128], o[:])
            if j % 4 == 3:
                sr = ssrow[(j // 4) % 2]
                cp(S, sr[:], ps_ss[0:1, 0:512])
                S.dma("pool", SSloc.ap()[0:1, t0 + (j - 3) * 128:t0 + (j + 1) * 128], sr[:])


def hgrn_stage(C, K, P, rows, B, SEQ, layer, lbl_d, gain_d, ydst, yrow0):
    S = C.S
    Pa = P.ap()
    CH = 32
    nch = SEQ // CH
    lbl = C.sb("hg_lbl", [128, 2], F32)
    S.dma("sp", lbl[:], lbl_d.ap()[:, :])
    gain = C.sb("hg_gain", [128, 1], F32)
    S.dma("sp", gain[:], gain_d.ap()[:, :])
    lb = C.sb("hg_lb", [128, 1], F32)
    oml = C.sb("hg_oml", [128, 1], F32)
    if layer == 0:
        memset(S, lb[:], 0.0)
    else:
        tt(S, lb[:], lbl[:, 1:2], lbl[:, 0:1], ALU.subtract)
        act(S, lb[:], lb[:], AF.Sigmoid)
    ts(S, oml[:], lb[:], -1.0, ALU.mult, 1.0, ALU.add)
    tq = C.sb("hg_tq", [128, SEQ], F32)
    tf = C.sb("hg_tf", [128, SEQ], F32)
    tk = C.sb("hg_tk", [128, SEQ], F32)
    tb = C.sb("hg_tb", [128, SEQ], F32)
    td = C.sb("hg_td", [128, SEQ], F32)
    msk = C.sb("hg_msk", [128, SEQ], BF16)
    qt = C.sb("hg_qt", [128, SEQ], BF16)
    kt = C.sb("hg_kt", [128, SEQ], BF16)
    qin = C.sb("hg_qin", [128, SEQ], BF16)
    vT = C.sb("hg_vT", [128, SEQ], BF16)
    ebl = C.sb("hg_ebl", [128, nch], F32)
    SF = C.sb("hg_SF", [128, 128], F32)
    Sb = C.sb("hg_Sb", [128, 128], BF16)
    kvt = [C.sb("hg_kvt%d" % i, [32, 512], BF16) for i in range(2)]
    AT = [C.sb("hg_AT%d" % i, [32, 32], BF16) for i in range(3)]
    gz = [C.sb("hg_gz%d" % i, [128, 512], F32) for i in range(2)]
    og = [C.sb("hg_og%d" % i, [128, 512], F32) for i in range(2)]
    osq = C.sb("hg_osq", [128, 512], BF16)
    rstd = C.sb("hg_rstd", [128, 512], F32)
    yo = [C.sb("hg_yo%d" % i, [128, 512], BF16) for i in range(2)]
    memset(S, msk[:], 1.0)
    memset(S, msk[:, :].rearrange("p (c l) -> p c l", l=CH)[:, :, 0:1], 0.0)
    v3 = lambda t: t[:, :].rearrange("p (c l) -> p c l", l=CH)
    for b in range(B):
        t0 = b * SEQ
        S.dma("sp", tq[:], Pa[rows["q"]:rows["q"] + 128, t0:t0 + SEQ])
        act(S, tq[:], tq[:], AF.Silu)
        S.dma("sp", tf[:], Pa[rows["f"]:rows["f"] + 128, t0:t0 + SEQ])
        act(S, tf[:], tf[:], AF.Sigmoid)
        ts(S, tf[:], tf[:], oml[:, 0:1], ALU.mult, lb[:, 0:1], ALU.add)
        ts(S, tk[:], tf[:], -1.0, ALU.mult, 1.0, ALU.add)
        act(S, tf[:], tf[:], AF.Ln)
        S.op("dve", lambda e: e.tensor_tensor_scan(out=tb[:], data0=msk[:], data1=tf[:], initial=0.0,
                                                   op0=ALU.mult, op1=ALU.add), R=[msk, tf], W=[tb])
        tt(S, v3(td), v3(tb), v3(tb)[:, :, CH - 1:CH].broadcast_to([128, nch, CH]), ALU.subtract, R=[tb], W=[td])
        act(S, tf[:], td[:], AF.Exp, scale=-1.0)
        act(S, td[:], td[:], AF.Exp)
        act(S, tb[:], tb[:], AF.Exp)
        tt(S, qt[:], tq[:], td[:], ALU.mult)
        tt(S, kt[:], tk[:], tf[:], ALU.mult)
        tt(S, qin[:], tq[:], tb[:], ALU.mult)
        cp(S, ebl[:], v3(tb)[:, :, CH - 1:CH].rearrange("p c l -> p (c l)"), R=[tb], W=[ebl])
        S.dma("sp", tk[:], Pa[rows["i"]:rows["i"] + 128, t0:t0 + SEQ])
        act(S, vT[:], tk[:], AF.Silu)
        memset(S, SF[:], 0.0)
        memset(S, Sb[:], 0.0)
        for j in range(nch):
            ck = slice(j * CH, (j + 1) * CH)
            if j % 2 == 0:
                ps_t = C.bank((j // 2) % 2)
                for jj in range(2):
                    c2 = slice((j + jj) * CH, (j + jj + 1) * CH)
                    mm(S, ps_t[0:32, jj * 256:jj * 256 + 128], kt[:, c2], K["idb"][:], R=[kt], W=[ps_t])
                    mm(S, ps_t[0:32, jj * 256 + 128:jj * 256 + 256], vT[:, c2], K["idb"][:], R=[vT], W=[ps_t])
                kv = kvt[(j // 2) % 2]
                cp(S, kv[:], ps_t[0:32, 0:512], eng="act")
            kv = kvt[(j // 2) % 2]
            ktok = kv[:, (j % 2) * 256:(j % 2) * 256 + 128]
            vtok = kv[:, (j % 2) * 256 + 128:(j % 2) * 256 + 256]
            ps_sc = C.bank(2 + j % 2)
            mm(S, ps_sc[0:32, 0:32], kt[:, ck], qt[:, ck], R=[kt, qt], W=[ps_sc])
            at = AT[j % 3]
            tt(S, at[:], ps_sc[0:32, 0:32], K["m01"][0:32, 0:32], ALU.mult)
            if j % 16 == 0:
                ps_o = C.bank(4 + (j // 16) % 2)
            oc = ps_o[:, (j % 16) * CH:(j % 16 + 1) * CH]
            mm(S, oc, vtok, at[:], start=True, stop=(j == 0), R=[kv, at], W=[ps_o])
            if j > 0:
                mm(S, oc, Sb[:], qin[:, ck], start=False, stop=True, R=[Sb, qin], W=[ps_o])
            ps_s = C.bank(6)
            mm(S, ps_s[:, 0:128], ktok, vtok, R=[kv], W=[ps_s])
            stt(S, SF[:], SF[:], ebl[:, j:j + 1], ps_s[:, 0:128], ALU.mult, ALU.add)
            cp(S, Sb[:], SF[:])
            if j % 16 == 15:
                g = j // 16
                c0 = t0 + g * 512
                gzz = gz[g % 2]
                S.dma("sp", gzz[:], Pa[rows["g"]:rows["g"] + 128, c0:c0 + 512])
                act(S, gzz[:], gzz[:], AF.Sigmoid)
                o_ = og[g % 2]
                tt(S, o_[:], ps_o[:, 0:512], gzz[:], ALU.mult)
                act(S, osq[:], o_[:], AF.Square)
                ps_n = C.bank(7)
                mm(S, ps_n[:, 0:512], K["ones_b"][:], osq[:], R=[osq], W=[ps_n])
                act(S, rstd[:], ps_n[:, 0:512], AF.Sqrt, bias=K["epsb"][:, 0:1], scale=1.0 / 128)
                S.op("dve", lambda e: e.reciprocal(out=rstd[:], in_=rstd[:]), R=[rstd], W=[rstd])
                y_ = yo[g % 2]
                stt(S, y_[:], o_[:], gain[:, 0:1], rstd[:], ALU.mult, ALU.mult)
                S.dma("pool", ydst.ap()[yrow0:yrow0 + 128, c0:c0 + 512], y_[:])


D_MODEL = 4096
D_FF = 11008
FFC = D_FF // NCORES
DC = D_MODEL // NCORES
NLOC = 1411
NSH = 170
PR = dict(q=0, f=128, i=256, g=384, fq=512, fk=640, fv=768, z=896, x=1024, Bm=1152, Cm=1280, ff=1408, dt=1409)


def in_cols(c):
    g = c // 4
    r = lambda a: list(range(a, a + 128))
    loc = (r(0 + 128 * c) + r(1024 + 128 * c) + r(2048 + 128 * c) + r(3072 + 128 * c) +
           r(8032 + 128 * c) + r(9056 + 128 * c) + r(10080 + 128 * c) +
           r(5456 + 128 * c) + r(6480 + 128 * c) + r(7504 + 128 * g) + r(7760 + 128 * g) +
           [11104 + c, 8016 + 2 * c, 8016 + 2 * c + 1])
    sh = [(j if j < 5456 else -1) for j in range(4096 + NSH * c, 4096 + NSH * (c + 1))]
    assert len(loc) == NLOC
    return np.array(loc + sh)


LAYER_INPUTS = [("win", [D_MODEL, NLOC + NSH]), ("wg", [2, D_MODEL, 1024]), ("wb", [D_MODEL, DC]), ("wo", [D_MODEL, DC]),
                ("wfg", [D_MODEL, FFC]), ("wfu", [D_MODEL, FFC]), ("wfd", [D_FF, DC]), ("fcw", [128, 33]),
                ("ang", [128, 4]), ("fng", [128, 4]), ("lbl", [128, 2]), ("hgn", [128, 1]),
                ("qn", [768]), ("kvn", [512]), ("wiq", [768, 1024]), ("wuq", [768, 128]), ("wuk", [512, 128]), ("wuv", [512, 128]),
                ("cw", [128, 12]), ("cb", [128, 3]), ("dtb", [1, 2]), ("alog", [1, 2]), ("dsk", [128, 1]), ("mgain", [128, 8]),
                ("fxb", [1, 1])]


def host_inputs(c, inp, B, SEQ, DEPTH):
    T = B * SEQ
    d = {}
    xT = inp["x"].reshape(T, D_MODEL).T
    d["xT"] = np.ascontiguousarray(xT[DC * c:DC * (c + 1)])
    d["fng"] = np.ascontiguousarray(inp["final_norm"][DC * c:DC * (c + 1)].reshape(4, 128).T)
    d["fnall"] = np.ascontiguousarray(inp["final_norm"])
    tbl, ixm = dsa_host_tables(c, B, SEQ)
    d["tbl"] = tbl
    d["ixmask"] = ixm
    g = c // 4
    cs = slice(DC * c, DC * (c + 1))
    hs = slice(128 * c, 128 * (c + 1))
    for l in range(DEPTH):
        p = "L%d_" % l
        ic = in_cols(c)
        wsl = inp["w_in"][l][:, np.maximum(ic, 0)].copy()
        wsl[:, ic < 0] = 0.0
        d[p + "win"] = np.ascontiguousarray(wsl)
        wg = inp["w_gate"][l]
        d[p + "wg"] = np.ascontiguousarray(np.stack([np.concatenate([wg[0][:, cs], wg[1][:, cs]], 1),
                                                      np.concatenate([wg[2][:, cs], wg[3][:, cs]], 1)], 0))
        wb = inp["w_branch"][l]
        d[p + "wb"] = np.ascontiguousarray(np.concatenate([wb[n][128 * cc:128 * (cc + 1), cs] for cc in range(NCORES) for n in range(4)], 0))
        d[p + "wo"] = np.ascontiguousarray(inp["w_out"][l][:, cs])
        fs = slice(FFC * c, FFC * (c + 1))
        d[p + "wfg"] = np.ascontiguousarray(inp["ffn_w_gate"][l][:, fs])
        d[p + "wfu"] = np.ascontiguousarray(inp["ffn_w_up"][l][:, fs])
        d[p + "wfd"] = np.ascontiguousarray(inp["ffn_w_down"][l][:, cs])
        fc = np.zeros((3, 11 * 128), np.float32)
        fc[:, :FFC] = inp["ffn_conv"][l][:, fs]
        d[p + "fcw"] = np.ascontiguousarray(fc.reshape(3, 11, 128).transpose(2, 1, 0).reshape(128, 33))
        d[p + "ang"] = np.ascontiguousarray(inp["attn_norm"][l][cs].reshape(4, 128).T)
        d[p + "fng"] = np.ascontiguousarray(inp["ffn_norm"][l][cs].reshape(4, 128).T)
        d[p + "lbl"] = np.ascontiguousarray(inp["hgrn_lb_logits"][:, hs].T)
        d[p + "hgn"] = np.ascontiguousarray(inp["hgrn_norm"][l][hs].reshape(128, 1))
        d[p + "qn"] = np.ascontiguousarray(inp["dsa_q_norm"][l])
        d[p + "kvn"] = np.ascontiguousarray(inp["dsa_kv_norm"][l])
        d[p + "wiq"] = np.ascontiguousarray(inp["dsa_w_iq"][l])
        d[p + "wuq"] = np.ascontiguousarray(inp["dsa_w_uq"][l][:, hs])
        d[p + "wuk"] = np.ascontiguousarray(inp["dsa_w_uk"][l][:, hs])
        d[p + "wuv"] = np.ascontiguousarray(inp["dsa_w_uv"][l][:, hs])
        cwf = inp["ssm_conv_w"][l]
        cbf = inp["ssm_conv_b"][l]
        Bs = slice(1024 + 128 * g, 1024 + 128 * (g + 1))
        Cs = slice(1280 + 128 * g, 1280 + 128 * (g + 1))
        cw = np.stack([cwf[:, hs], cwf[:, Bs], cwf[:, Cs]], 0)
        d[p + "cw"] = np.ascontiguousarray(cw.transpose(2, 0, 1).reshape(128, 12))
        d[p + "cb"] = np.ascontiguousarray(np.stack([cbf[hs], cbf[Bs], cbf[Cs]], 1))
        d[p + "dtb"] = np.ascontiguousarray(inp["ssm_dt_bias"][l][2 * c:2 * c + 2].reshape(1, 2))
        d[p + "alog"] = np.ascontiguousarray(inp["ssm_a_log"][l][2 * c:2 * c + 2].reshape(1, 2))
        d[p + "dsk"] = np.ascontiguousarray(np.repeat(inp["ssm_d"][l][2 * c:2 * c + 2], 64).reshape(128, 1))
        d[p + "mgain"] = np.ascontiguousarray(inp["ssm_norm"][l].reshape(8, 128).T)
        d[p + "fxb"] = np.ascontiguousarray(inp["fox_f_bias"][l][c:c + 1].reshape(1, 1))
    return {k: np.asarray(v, dtype=(np.int32 if k == "tbl" else np.float32)) for k, v in d.items()}


def build_model(B, SEQ, DEPTH, topk):
    T = B * SEQ
    nslot = SEQ // 128 // 4
    nc = bass.Bass("TRN2", target_bir_lowering=False)
    ein = lambda n, s, dt=F32: nc.dram_tensor(n, list(s), dt, kind="ExternalInput")
    xT_in = ein("xT", [DC, T])
    fng = ein("fng", [128, 4])
    fnall = ein("fnall", [D_MODEL])
    tbl = ein("tbl", [1, 16], I32)
    ixmask = ein("ixmask", [128, 512])
    class Lazy(dict):
        def __init__(self, l):
            self.l = l
            self.shapes = dict(LAYER_INPUTS)

        def __missing__(self, n):
            self[n] = ein("L%d_%s" % (self.l, n), self.shapes[n])
            return self[n]
    LI = [Lazy(l) for l in range(DEPTH)]
    outT = nc.dram_tensor("outT", [DC, T], F32, kind="ExternalOutput")
    with ExitStack() as es:
        C = Ctx(nc, es)
        S = C.S
        S.kstop = KSTOP
        S.nflush = 0
        dr = lambda name, shape, dt, ag=False: C.dram(name, shape, dt, ag=ag)
        Xloc = dr("Xloc", [DC, T], F32)
        X2loc = dr("X2loc", [DC, T], F32)
        Hloc = dr("Hloc", [DC, T], BF16)
        NSloc, NSall = dr("NSloc", [1, T], F32), dr("NSall", [NCORES, T], F32)
        H = dr("H", [D_MODEL, T], BF16)
        Ploc, SHloc, SH = dr("Ploc", [NLOC, T], F32), dr("SHloc", [NSH, T], F32, ag=True), dr("SH", [NSH * NCORES, T], F32)
        G = dr("G", [4 * DC, T], BF16)
        MBloc, MBall = dr("MBloc", [nslot * 128, SEQ], BF16, ag=True), dr("MBall", [NCORES * nslot * 128, SEQ], BF16)
        Yloc, Y = dr("Yloc", [DC, T], BF16, ag=True), dr("Y", [D_MODEL, T], BF16)
        SSloc, SSall = dr("SSloc", [1, T], F32, ag=True), dr("SSall", [NCORES, T], F32)
        MGloc, MG = dr("MGloc", [DC, T], BF16, ag=True), dr("MG", [D_MODEL, T], BF16)
        Sg = dr("Sg", [FFC, T], BF16)
        ACTloc, ACT = dr("ACTloc", [FFC, T], BF16, ag=True), dr("ACT", [D_FF, T], BF16)

        def body():
            C.es = es
            C._banks = {}
            K = make_consts(C)
            S.flush()
            with C.stage():
                for r0 in range(0, DC, 128):
                    S.dma("sp", Xloc.ap()[r0:r0 + 128, :], xT_in.ap()[r0:r0 + 128, :], R=[], W=[])
                S.flush()

            def norm_to_H(xloc_, gain_in):
                with C.stage():
                    norm_part_stage(C, K, xloc_, NSloc, T)
                    S.allgather(NSloc, NSall)
                    S.flush()
                with C.stage():
                    norm_apply_stage(C, K, xloc_, NSall, gain_in, Hloc, T, BF16)
                    S.allgather(Hloc, H)
                    S.flush()

            for l in range(DEPTH):
                L = LI[l]
                norm_to_H(Xloc, L["ang"])
                with C.stage():
                    W = load_weights(C, L["win"], D_MODEL, NLOC + NSH)
                    ob = [C.sb("a1ob%d" % i, [128, 512], F32) for i in range(3)]
                    cnt = [0]

                    def epi(ps, m, msz, j):
                        b_ = ob[cnt[0] % 3]
                        cnt[0] += 1
                        cp(S, b_[0:msz, :], ps[0:msz, :], eng=("act" if cnt[0] % 2 else "dve"))
                        r0, r1 = m * 128, m * 128 + msz
                        cs_ = slice(j * 512, (j + 1) * 512)
                        if r1 <= NLOC:
                            S.dma("pool", Ploc.ap()[r0:r1, cs_], b_[0:msz, :])
                        elif r0 >= NLOC:
                            S.dma("pool", SHloc.ap()[r0 - NLOC:r1 - NLOC, cs_], b_[0:msz, :])
                        else:
                            S.dma("pool", Ploc.ap()[r0:NLOC, cs_], b_[0:NLOC - r0, :])
                            S.dma("pool", SHloc.ap()[0:r1 - NLOC, cs_], b_[NLOC - r0:msz, :])
                    dense_stage(C, H, D_MODEL, T, W, NLOC + NSH, epi)
                    S.allgather(SHloc, SH)
                    S.flush()
                for half in range(2):
                    with C.stage():
                        W = load_weights(C, L["wg"].ap()[half], D_MODEL, 1024)
                        ob = [C.sb("a2ob%d" % i, [128, 512], BF16) for i in range(3)]
                        cnt = [0]

                        def epi(ps, m, msz, j, half=half):
                            b_ = ob[cnt[0] % 3]
                            cnt[0] += 1
                            act(S, b_[:], ps[:, 0:512], AF.Sigmoid)
                            S.dma("pool", G.ap()[half * 1024 + m * 128:half * 1024 + (m + 1) * 128, j * 512:(j + 1) * 512], b_[:])
                        dense_stage(C, H, D_MODEL, T, W, 1024, epi)
                        S.flush()
                with C.stage():
                    dsa_index_stage(C, K, SH, B, SEQ, topk, L["wiq"], L["qn"], tbl, ixmask, MBloc)
                    S.allgather(MBloc, MBall)
                    S.flush()
                with C.stage():
                    hgrn_stage(C, K, Ploc, PR, B, SEQ, l, L["lbl"], L["hgn"], Yloc, 0)
                    S.flush()
                with C.stage():
                    mamba_stage(C, K, Ploc, PR, B, SEQ, L["cw"], L["cb"], L["dtb"], L["alog"], L["dsk"], Yloc, 256, SSloc)
                    S.flush()
                with C.stage():
                    fox_head(C, K, Ploc, PR["fq"], PR["fk"], PR["fv"], PR["ff"], L["fxb"].ap()[:, :], Yloc, 384, B, SEQ)
                    S.flush()
                with C.stage():
                    dsa_attn_stage(C, K, B, SEQ, SH, MBall, L["wuq"], L["wuk"], L["wuv"], L["qn"], L["kvn"], Yloc, 128)
                    S.allgather(Yloc, Y)
                    S.allgather(SSloc, SSall)
                    S.flush()
                with C.stage():
                    stage_B(C, K, Y, SSall, G, L["wb"], L["mgain"], MGloc, T)
                    S.allgather(MGloc, MG)
                    S.flush()
                with C.stage():
                    W = load_weights(C, L["wo"], D_MODEL, DC)
                    residual_dense(C, MG, D_MODEL, T, W, Xloc, X2loc, 512)
                    S.flush()
                norm_to_H(X2loc, L["fng"])
                with C.stage():
                    stage_D1(C, H, L["wfg"], L["fcw"], Sg, T, SEQ)
                    S.flush()
                with C.stage():
                    W = load_weights(C, L["wfu"], D_MODEL, FFC)
                    sgb = [C.sb("d2sg%d" % i, [128, 512], BF16) for i in range(3)]
                    ob = [C.sb("d2ob%d" % i, [128, 512], BF16) for i in range(3)]
                    cnt = [0]

                    def epi(ps, m, msz, j):
                        i_ = cnt[0] % 3
                        cnt[0] += 1
                        S.dma("sp", sgb[i_][0:msz, :], Sg.ap()[m * 128:m * 128 + msz, j * 512:(j + 1) * 512])
                        tt(S, ob[i_][0:msz, :], ps[0:msz, 0:512], sgb[i_][0:msz, :], ALU.mult)
                        S.dma("pool", ACTloc.ap()[m * 128:m * 128 + msz, j * 512:(j + 1) * 512], ob[i_][0:msz, :])
                    dense_stage(C, H, D_MODEL, T, W, FFC, epi)
                    S.allgather(ACTloc, ACT)
                    S.flush()
                with C.stage():
                    W = load_weights(C, L["wfd"], D_FF, DC)
                    residual_dense(C, ACT, D_FF, T, W, X2loc, Xloc, 256)
                    S.flush()
            with C.stage():
                norm_part_stage(C, K, Xloc, NSloc, T)
                S.allgather(NSloc, NSall)
                S.flush()
            with C.stage():
                norm_apply_stage(C, K, Xloc, NSall, fng, outT, T, F32)
                S.flush()
        try:
            body()
        except StopBuild:
            with C.stage():
                for r0 in range(0, DC, 128):
                    S.dma("sp", outT.ap()[r0:r0 + 128, :], xT_in.ap()[r0:r0 + 128, :], R=[], W=[])
                S.kstop = 0
                S.flush()
    return nc


def residual_dense(C, src, Kdim, T, W, xres, dst, NT):
    S = C.S
    xb = [C.sb("rdx%d" % i, [128, NT], F32) for i in range(3)]
    ob = [C.sb("rdo%d" % i, [128, NT], F32) for i in range(3)]
    cnt = [0]

    def epi(ps, m, msz, j):
        i_ = cnt[0] % 3
        cnt[0] += 1
        S.dma("sp", xb[i_][:], xres.ap()[m * 128:(m + 1) * 128, j * NT:(j + 1) * NT])
        tt(S, ob[i_][:], ps[:, 0:NT], xb[i_][:], ALU.add)
        S.dma("pool", dst.ap()[m * 128:(m + 1) * 128, j * NT:(j + 1) * NT], ob[i_][:])
    dense_stage(C, src, Kdim, T, W, DC, epi, NT=NT)


def stage_B(C, K, Y, SSall, G, wb_d, mgain_d, MGloc, T, NT=512):
    S = C.S
    W = load_weights(C, wb_d, D_MODEL, DC, name="Wb")
    mgain = C.sb("b_mgain", [128, 8], F32)
    S.dma("sp", mgain[:], mgain_d.ap()[:, :])
    yv = Y.ap().rearrange("(kc p) t -> p kc t", p=128)
    abuf = [C.sb("b_abuf%d" % i, [128, 32, NT], BF16) for i in range(2)]
    ssg = [C.sb("b_ss%d" % i, [4, NT], F32) for i in range(2)]
    rstd = [C.sb("b_rstd%d" % i, [128, NT], F32) for i in range(2)]
    gb = [C.sb("b_g%d" % i, [128, NT], BF16) for i in range(4)]
    acc = [C.sb("b_acc%d" % i, [128, NT], F32) for i in range(2)]
    tmpb = [C.sb("b_tmp%d" % i, [128, NT], F32) for i in range(2)]
    ob = [C.sb("b_ob%d" % i, [128, NT], BF16) for i in range(2)]
    gi = 0
    pi = 0
    for j in range(T // NT):
        a = abuf[j % 2]
        cs_ = slice(j * NT, (j + 1) * NT)
        S.dma("sp", a[:, 0:16, :], yv[:, 0:16, cs_], W=[(a.name, 0)])
        S.dma("sp", a[:, 16:32, :], yv[:, 16:32, cs_], W=[(a.name, 1)])
        for g in range(2):
            S.dma("sp", ssg[g][:], SSall.ap()[4 * g:4 * g + 4, cs_])
            ps = C.bank(6 + g)
            mm(S, ps[:, 0:NT], K["ones_f"][0:4, 0:128], ssg[g][:], W=[ps])
            act(S, rstd[g][:], ps[:, 0:NT], AF.Sqrt, bias=K["epsb"][:, 0:1], scale=1.0 / 512)
            S.op("dve", lambda e, g=g: e.reciprocal(out=rstd[g][:], in_=rstd[g][:]), R=[rstd[g]], W=[rstd[g]])
        for cc in range(NCORES):
            q = cc * 4 + 2
            stt(S, a[:, q, :], a[:, q, :], mgain[:, cc:cc + 1], rstd[cc // 4][:], ALU.mult, ALU.mult,
                R=[(a.name, q // 16), mgain, rstd[cc // 4]], W=[(a.name, q // 16)])
        for m in range(DC // 128):
            ac = acc[m % 2]
            for n in range(4):
                gt = gb[gi % 4]
                gi += 1
                S.dma("sp", gt[:], G.ap()[n * DC + m * 128:n * DC + (m + 1) * 128, cs_])
                ps = C.bank(pi % 4)
                pi += 1
                for cc in range(NCORES):
                    q = cc * 4 + n
                    mm(S, ps[:, 0:NT], W[:, q, m * 128:(m + 1) * 128], a[:, q, :], start=(cc == 0), stop=(cc == NCORES - 1),
                       R=[(a.name, q // 16), (W.name, q)], W=[ps])
                if n == 0:
                    tt(S, ac[:], ps[:, 0:NT], gt[:], ALU.mult)
                else:
                    tb_ = tmpb[n % 2]
                    tt(S, tb_[:], ps[:, 0:NT], gt[:], ALU.mult)
                    tt(S, ac[:], ac[:], tb_[:], ALU.add, eng="pool")
            o = ob[m % 2]
            cp(S, o[:], ac[:], eng="act")
            S.dma("pool", MGloc.ap()[m * 128:(m + 1) * 128, cs_], o[:])


def stage_D1(C, H, wfg_d, fcw_d, Sg, T, SEQ, NT=512):
    S = C.S
    W = load_weights(C, wfg_d, D_MODEL, FFC, name="Wfg")
    fcw = C.sb("d1_fcw", [128, 33], F32)
    S.dma("sp", fcw[:], fcw_d.ap()[:, :])
    MT = (FFC + 127) // 128
    carry = C.sb("d1_carry", [128, MT, 2], F32)
    gx = [C.sb("d1_gx%d" % i, [128, 2 + NT], F32) for i in range(3)]
    cv = [C.sb("d1_cv%d" % i, [128, NT], F32) for i in range(2)]
    ob = [C.sb("d1_ob%d" % i, [128, NT], BF16) for i in range(3)]
    cnt = [0]

    def epi(ps, m, msz, j):
        i_ = cnt[0] % 3
        cnt[0] += 1
        g_ = gx[i_]
        if (j * NT) % SEQ == 0:
            memset(S, g_[:, 0:2], 0.0)
        else:
            cp(S, g_[:, 0:2], carry[:, m, :], R=[(carry.name, m)], W=[g_])
        cp(S, g_[0:msz, 2:2 + NT], ps[0:msz, 0:NT], eng="act")
        cp(S, carry[:, m, :], g_[:, NT:NT + 2], R=[g_], W=[(carry.name, m)])
        c_ = cv[i_ % 2]
        ts(S, c_[:], g_[:, 2:2 + NT], fcw[:, 3 * m + 2:3 * m + 3], ALU.mult)
        stt(S, c_[:], g_[:, 1:1 + NT], fcw[:, 3 * m + 1:3 * m + 2], c_[:], ALU.mult, ALU.add)
        stt(S, c_[:], g_[:, 0:NT], fcw[:, 3 * m:3 * m + 1], c_[:], ALU.mult, ALU.add)
        o = ob[i_]
        act(S, o[:], c_[:], AF.Silu)
        S.dma("pool", Sg.ap()[m * 128:m * 128 + msz, j * NT:(j + 1) * NT], o[0:msz, :])
    for i_ in range(3):
        memset(S, gx[i_][:], 0.0)
    dense_stage(C, H, D_MODEL, T, W, FFC, epi, NT=NT)


def final_stage(C, K, X, Xloc, fnall_d, fng_d, outT, T, NT=256):
    S = C.S
    KC = D_MODEL // 128
    xv = X.ap().rearrange("(kc p) t -> p kc t", p=128)
    xl = Xloc.ap().rearrange("(kc p) t -> p kc t", p=128)
    ov = outT.ap().rearrange("(kc p) t -> p kc t", p=128)
    gain = C.sb("f_gain", [128, 4], F32)
    S.dma("sp", gain[:], fng_d.ap()[:, :])
    xb = [C.sb("f_xb%d" % i, [128, KC, NT], F32) for i in range(2)]
    xo = [C.sb("f_xo%d" % i, [128, 4, NT], F32) for i in range(2)]
    sq = C.sb("f_sq", [128, KC, NT], BF16)
    ob = [C.sb("f_ob%d" % i, [128, 4, NT], F32) for i in range(2)]
    rstd = C.sb("f_rstd", [128, NT], F32)
    h2 = KC // 2
    for j in range(T // NT):
        x = xb[j % 2]
        cs_ = slice(j * NT, (j + 1) * NT)
        S.dma("sp", x[:, 0:h2, :], xv[:, 0:h2, cs_], W=[(x.name, 0)])
        S.dma("sp", x[:, h2:KC, :], xv[:, h2:KC, cs_], W=[(x.name, 1)])
        xo_ = xo[j % 2]
        S.dma("sp", xo_[:], xl[:, :, cs_])
        ps = C.bank(j % 2)
        for k in range(KC):
            act(S, sq[:, k, :], x[:, k, :], AF.Square, R=[(x.name, 0 if k < h2 else 1)], W=[(sq.name, k)])
            mm(S, ps[:, 0:NT], K["ones_b"][:], sq[:, k, :], start=(k == 0), stop=(k == KC - 1), R=[(sq.name, k)], W=[ps])
        act(S, rstd[:], ps[:, 0:NT], AF.Sqrt, bias=K["epsb"][:, 0:1], scale=1.0 / D_MODEL)
        S.op("dve", lambda e: e.reciprocal(out=rstd[:], in_=rstd[:]), R=[rstd], W=[rstd])
        o = ob[j % 2]
        for k in range(4):
            stt(S, o[:, k, :], xo_[:, k, :], gain[:, k:k + 1], rstd[:], ALU.mult, ALU.mult, R=[xo_, gain, rstd], W=[(o.name, k)])
        S.dma("pool", ov[:, :, cs_], o[:], R=[(o.name, k) for k in range(4)], W=[])


def norm_part_stage(C, K, xloc, NSloc, T, NT=512):
    S = C.S
    xl = xloc.ap().rearrange("(kc p) t -> p kc t", p=128)
    xb = [C.sb("np_x%d" % i, [128, 4, NT], F32) for i in range(2)]
    sq = [C.sb("np_sq%d" % i, [128, 4, NT], BF16) for i in range(2)]
    sr = [C.sb("np_sr%d" % i, [1, NT], F32) for i in range(2)]
    for j in range(T // NT):
        x = xb[j % 2]
        cs_ = slice(j * NT, (j + 1) * NT)
        S.dma("sp", x[:], xl[:, :, cs_])
        q = sq[j % 2]
        act(S, q[:], x[:], AF.Square)
        ps = C.bank(j % 2)
        for k in range(4):
            mm(S, ps[:, 0:NT], K["ones_b"][:], q[:, k, :], start=(k == 0), stop=(k == 3), R=[q], W=[ps])
        r = sr[j % 2]
        cp(S, r[:], ps[0:1, 0:NT])
        S.dma("pool", NSloc.ap()[0:1, cs_], r[:])


def norm_apply_stage(C, K, xloc, NSall, gain_d, dst, T, out_dt, NT=512):
    S = C.S
    xl = xloc.ap().rearrange("(kc p) t -> p kc t", p=128)
    dv = dst.ap().rearrange("(kc p) t -> p kc t", p=128)
    gain = C.sb("na_gain", [128, 4], F32)
    S.dma("sp", gain[:], gain_d.ap()[:, :])
    xb = [C.sb("na_x%d" % i, [128, 4, NT], F32) for i in range(2)]
    nsb = [C.sb("na_ns%d" % i, [NCORES, NT], F32) for i in range(2)]
    ob = [C.sb("na_o%d" % i, [128, 4, NT], out_dt) for i in range(2)]
    rstd = [C.sb("na_rstd%d" % i, [128, NT], F32) for i in range(2)]
    for j in range(T // NT):
        x = xb[j % 2]
        cs_ = slice(j * NT, (j + 1) * NT)
        S.dma("sp", x[:], xl[:, :, cs_])
        n_ = nsb[j % 2]
        S.dma("sp", n_[:], NSall.ap()[:, cs_])
        ps = C.bank(j % 2)
        mm(S, ps[:, 0:NT], K["ones_f"][0:NCORES, 0:128], n_[:], W=[ps])
        rs = rstd[j % 2]
        act(S, rs[:], ps[:, 0:NT], AF.Sqrt, bias=K["epsb"][:, 0:1], scale=1.0 / D_MODEL)
        S.op("dve", lambda e, rs=rs: e.reciprocal(out=rs[:], in_=rs[:]), R=[rs], W=[rs])
        o = ob[j % 2]
        for k in range(4):
            stt(S, o[:, k, :], x[:, k, :], gain[:, k:k + 1], rs[:], ALU.mult, ALU.mult, R=[x, gain, rs], W=[(o.name, k)])
        S.dma("pool", dv[:, :, cs_], o[:], R=[(o.name, k) for k in range(4)], W=[])


_CACHE = {}


def kernel(**inputs):
    B, SEQ, DEPTH = 2, 4096, 2
    inp = {k: np.asarray(v) for k, v in inputs.items()}
    if "nc" not in _CACHE:
        _CACHE["nc"] = build_model(B, SEQ, DEPTH, min(256, SEQ // 4))
    nc = _CACHE["nc"]
    in_maps = [host_inputs(c, inp, B, SEQ, DEPTH) for c in range(NCORES)]
    res = run_bass_kernel_spmd(nc, in_maps, core_ids=list(range(NCORES)))
    outT = np.concatenate([res.results[c]["outT"] for c in range(NCORES)], 0)
    return np.ascontiguousarray(outT.T).reshape(B, SEQ, D_MODEL).astype(np.float32)
```
